# Optimizing a Trainium2 kernel written in Bass

```python
import math
import jax, jax.numpy as jnp
from jax import lax
import numpy as np

D_MODEL = 1024
BATCH = 8
SEQ = 2048
DEPTH = 1

ATTN_WIDTH = 512
HEAD_DIM = 64
N_DIFF_HEADS = ATTN_WIDTH // (2 * HEAD_DIM)
CONV_WIDTH = 512
CONV_KERNEL = 31
N_BUCKETS = 32
MAX_DISTANCE = 128
Q_BLOCK = 128
LN_EPS = 1e-5
DEEPNORM_ALPHA = (2.0 * DEPTH) ** 0.25
DEEPNORM_BETA = (8.0 * DEPTH) ** -0.25
D_IN = 4 * ATTN_WIDTH + 3 * CONV_WIDTH + 2 * D_MODEL
SPLITS = (ATTN_WIDTH, 2 * ATTN_WIDTH, 3 * ATTN_WIDTH, 4 * ATTN_WIDTH,
          4 * ATTN_WIDTH + 2 * CONV_WIDTH, 4 * ATTN_WIDTH + 3 * CONV_WIDTH)

kernel_name = "hybrid_diffattn_conformer_gated_deepnorm"


def layer_norm(x, g, b):
    xf = x.astype(jnp.float32)
    mu = xf.mean(-1, keepdims=True)
    var = jnp.square(xf - mu).mean(-1, keepdims=True)
    return ((xf - mu) * lax.rsqrt(var + LN_EPS) * g + b).astype(x.dtype)


def rms_norm(x, g):
    xf = x.astype(jnp.float32)
    return (xf * lax.rsqrt(jnp.mean(xf * xf, -1, keepdims=True) + LN_EPS) * g).astype(x.dtype)


def t5_causal_bucket(q_pos, k_pos):
    n = jnp.maximum(q_pos[:, None] - k_pos[None, :], 0)
    max_exact = N_BUCKETS // 2
    nf = jnp.maximum(n, 1).astype(jnp.float32)
    large = max_exact + (jnp.log(nf / max_exact) / math.log(MAX_DISTANCE / max_exact)
                         * (N_BUCKETS - max_exact)).astype(jnp.int32)
    large = jnp.minimum(large, N_BUCKETS - 1)
    return jnp.where(n < max_exact, n, large)


def diff_attention(q, k, v, lam, rel_bias):
    B, S = q.shape[0], q.shape[1]
    nblk = S // Q_BLOCK
    scale = HEAD_DIM ** -0.5
    qb = q.reshape(B, nblk, Q_BLOCK, N_DIFF_HEADS, 2, HEAD_DIM).transpose(1, 0, 2, 3, 4, 5)
    k_pos = jnp.arange(S)

    def block(args):
        i, q_i = args
        q_pos = i * Q_BLOCK + jnp.arange(Q_BLOCK)
        s = jnp.einsum('bqhcd,bkhcd->bhcqk', q_i, k).astype(jnp.float32) * scale
        bias = rel_bias.astype(jnp.float32)[t5_causal_bucket(q_pos, k_pos)]
        bias = bias.transpose(2, 0, 1)[None, :, None]
        mask = k_pos[None, :] <= q_pos[:, None]
        s = jnp.where(mask, s + bias, -1e30)
        p = jax.nn.softmax(s, axis=-1)
        p = p[:, :, 0] - lam * p[:, :, 1]
        return jnp.einsum('bhqk,bkhe->bqhe', p.astype(v.dtype), v)

    o = lax.map(block, (jnp.arange(nblk), qb))
    return o.transpose(1, 0, 2, 3, 4).reshape(B, S, N_DIFF_HEADS, 2 * HEAD_DIM)


def causal_depthwise_conv(u, w, b):
    y = lax.conv_general_dilated(u, w[:, None, :], window_strides=(1,),
                                 padding=[(CONV_KERNEL - 1, 0)],
                                 dimension_numbers=('NWC', 'WIO', 'NWC'),
                                 feature_group_count=u.shape[-1])
    return y + b


def setup_inputs(seed: int = 0) -> dict:
    key = jax.random.key(seed)
    ks = jax.random.split(key, 16)
    f32 = jnp.float32
    x = jax.random.normal(ks[0], (BATCH, SEQ, D_MODEL), f32)
    w_in = jax.random.normal(ks[1], (DEPTH, D_MODEL, D_IN), f32) * D_MODEL ** -0.5
    col_scale = jnp.ones((D_IN,), f32).at[2 * ATTN_WIDTH:3 * ATTN_WIDTH].set(DEEPNORM_BETA)
    w_in = w_in * col_scale
    lambda_qk = jax.random.normal(ks[2], (DEPTH, 4, HEAD_DIM), f32) * 0.1
    subln_w = 1.0 + 0.02 * jax.random.normal(ks[3], (DEPTH, 2 * HEAD_DIM), f32)
    w_attn_proj = jax.random.normal(ks[4], (DEPTH, ATTN_WIDTH, D_MODEL), f32) * ATTN_WIDTH ** -0.5 * DEEPNORM_BETA
    conv_w = jax.random.normal(ks[5], (DEPTH, CONV_KERNEL, CONV_WIDTH), f32) * CONV_KERNEL ** -0.5
    conv_b = 0.02 * jax.random.normal(ks[6], (DEPTH, CONV_WIDTH), f32)
    conv_ln_g = 1.0 + 0.02 * jax.random.normal(ks[7], (DEPTH, CONV_WIDTH), f32)
    conv_ln_b = 0.02 * jax.random.normal(ks[8], (DEPTH, CONV_WIDTH), f32)
    w_conv_proj = jax.random.normal(ks[9], (DEPTH, CONV_WIDTH, D_MODEL), f32) * CONV_WIDTH ** -0.5 * DEEPNORM_BETA
    b_conv_proj = 0.02 * jax.random.normal(ks[10], (DEPTH, D_MODEL), f32)
    w_out = jax.random.normal(ks[11], (DEPTH, D_MODEL, D_MODEL), f32) * D_MODEL ** -0.5 * DEEPNORM_BETA
    post_ln_g = 1.0 + 0.02 * jax.random.normal(ks[12], (DEPTH, D_MODEL), f32)
    post_ln_b = 0.02 * jax.random.normal(ks[13], (DEPTH, D_MODEL), f32)
    rel_bias = 0.5 * jax.random.normal(ks[14], (N_BUCKETS, N_DIFF_HEADS), f32)
    return {"x": x, "w_in": w_in, "lambda_qk": lambda_qk, "subln_w": subln_w,
            "w_attn_proj": w_attn_proj, "conv_w": conv_w, "conv_b": conv_b,
            "conv_ln_g": conv_ln_g, "conv_ln_b": conv_ln_b, "w_conv_proj": w_conv_proj,
            "b_conv_proj": b_conv_proj, "w_out": w_out, "post_ln_g": post_ln_g,
            "post_ln_b": post_ln_b, "rel_bias": rel_bias}


def reference(x, w_in, lambda_qk, subln_w, w_attn_proj, conv_w, conv_b, conv_ln_g, conv_ln_b,
              w_conv_proj, b_conv_proj, w_out, post_ln_g, post_ln_b, rel_bias):
    B, S, _ = x.shape
    h = x
    for l in range(DEPTH):
        lambda_init = 0.8 - 0.6 * math.exp(-0.3 * l)
        proj = h @ w_in[l]
        q, k, v, g_attn, glu, g_conv, gates = jnp.split(proj, SPLITS, axis=-1)

        lq = lambda_qk[l].astype(jnp.float32)
        lam = jnp.exp(jnp.sum(lq[0] * lq[1])) - jnp.exp(jnp.sum(lq[2] * lq[3])) + lambda_init
        q = q.reshape(B, S, N_DIFF_HEADS, 2, HEAD_DIM)
        k = k.reshape(B, S, N_DIFF_HEADS, 2, HEAD_DIM)
        v = v.reshape(B, S, N_DIFF_HEADS, 2 * HEAD_DIM)
        o = diff_attention(q, k, v, lam, rel_bias)
        o = rms_norm(o, subln_w[l]) * (1.0 - lambda_init)
        o = o.reshape(B, S, ATTN_WIDTH) * jax.nn.silu(g_attn)
        y_attn = o @ w_attn_proj[l]

        a, bgate = jnp.split(glu, 2, axis=-1)
        u = a * jax.nn.sigmoid(bgate)
        u = causal_depthwise_conv(u, conv_w[l], conv_b[l])
        u = layer_norm(u, conv_ln_g[l], conv_ln_b[l])
        u = jax.nn.silu(u) * jax.nn.silu(g_conv)
        y_conv = u @ w_conv_proj[l] + b_conv_proj[l]

        gate_attn, gate_conv = jnp.split(gates, 2, axis=-1)
        merged = jax.nn.sigmoid(gate_attn) * y_attn + jax.nn.sigmoid(gate_conv) * y_conv
        out = merged @ w_out[l]
        h = layer_norm(DEEPNORM_ALPHA * h + out, post_ln_g[l], post_ln_b[l])
    return h
```

```python
import math
from contextlib import ExitStack

import numpy as np
import concourse.bass as bass
import concourse.mybir as mybir
from concourse.ap import AP
from concourse.bass_utils import run_bass_kernel_spmd

F32 = mybir.dt.float32
BF16 = mybir.dt.bfloat16
AF = mybir.ActivationFunctionType
ALU = mybir.AluOpType

N_CORES = 8
SEQ = 2048
D_MODEL = 1024
CONV_K = 31
LN_EPS = 1e-5
LAMBDA_INIT = 0.8 - 0.6 * math.exp(-0.3 * 0)
ALPHA = (2.0 * 1) ** 0.25
NEG = -30000.0
_DBG = {}


class _Op:
    __slots__ = ("eng", "fn", "deps", "dma", "signal", "sem_key", "sig_val", "prev_ring")

    def __init__(self, eng, fn, dma):
        self.eng = eng
        self.fn = fn
        self.dma = dma
        self.deps = []
        self.signal = False
        self.sem_key = None
        self.sig_val = 0
        self.prev_ring = None


class Sched:
    ENGS = ("pe", "act", "dve", "pool", "sp")
    RING = 24

    def __init__(self, nc):
        self.nc = nc
        self.q = {e: [] for e in self.ENGS}
        self.last_write = {}
        self.readers = {}
        self.bank_last = {}
        self.phase_deps = []

    def add(self, eng, fn, reads=(), writes=(), banks=(), dma=False, after=()):
        op = _Op(eng, fn, dma)
        deps = []
        for k in after:
            lw = self.last_write.get(k)
            if lw is not None:
                deps.append(lw)
            deps.extend(self.readers.get(k, ()))
        for r in reads:
            lw = self.last_write.get(r)
            if lw is not None:
                deps.append(lw)
        for w in writes:
            lw = self.last_write.get(w)
            if lw is not None:
                deps.append(lw)
            deps.extend(self.readers.get(w, ()))
        for b in banks:
            lt = self.bank_last.get(b)
            if lt is not None and lt.eng != eng:
                deps.append(lt)
        if eng != "pe":
            deps.extend(self.phase_deps)
        seen = set()
        for p in deps:
            if id(p) in seen or p is op:
                continue
            seen.add(id(p))
            if (not p.dma) and p.eng == eng and eng == "pe":
                continue
            p.signal = True
            op.deps.append(p)
        for r in reads:
            self.readers.setdefault(r, []).append(op)
        for w in writes:
            self.last_write[w] = op
            self.readers[w] = []
        for b in banks:
            self.bank_last[b] = op
        self.q[eng].append(op)
        return op

    def soft_barrier(self):
        lasts = []
        for e in self.ENGS:
            ops = self.q[e]
            for op in reversed(ops):
                if not op.dma:
                    lasts.append(op)
                    break
            nd = 0
            for op in reversed(ops):
                if op.dma:
                    lasts.append(op)
                    nd += 1
                    if nd >= self.RING:
                        break
        for p in lasts:
            p.signal = True
        self.phase_deps = lasts

    def barrier(self):
        lasts = []
        for e in self.ENGS:
            ops = self.q[e]
            lc = None
            for op in reversed(ops):
                if not op.dma:
                    lc = op
                    break
            if lc is not None:
                lasts.append(lc)
            nd = 0
            for op in reversed(ops):
                if op.dma:
                    lasts.append(op)
                    nd += 1
                    if nd >= self.RING:
                        break
        for e in self.ENGS:
            op = _Op(e, lambda eng: eng.nop(), False)
            for p in lasts:
                if (not p.dma) and p.eng == e and e == "pe":
                    continue
                p.signal = True
                op.deps.append(p)
            self.q[e].append(op)

    def emit(self):
        nc = self.nc
        with ExitStack() as es:
            sems = {}
            for e in ("pe", "act", "dve", "pool", "sp"):
                sems[("c", e)] = es.enter_context(nc.semaphore("c_" + e))
            for e in ("sp", "pool"):
                for i in range(self.RING):
                    sems[("d", e, i)] = es.enter_context(nc.semaphore("d_%s_%d" % (e, i)))
            for e in self.ENGS:
                cnt = 0
                nd = 0
                ring_last = {}
                for op in self.q[e]:
                    if op.dma:
                        slot = nd % self.RING
                        op.sem_key = ("d", e, slot)
                        op.sig_val = 16 * (nd // self.RING + 1)
                        op.prev_ring = ring_last.get(slot)
                        ring_last[slot] = op
                        nd += 1
                    elif op.signal:
                        cnt += 1
                        op.sem_key = ("c", e)
                        op.sig_val = cnt
            block = es.enter_context(nc.Block())

            def run(ename):
                def body(eng):
                    waited = {}
                    for op in self.q[ename]:
                        need = {}
                        for p in op.deps:
                            if need.get(p.sem_key, 0) < p.sig_val:
                                need[p.sem_key] = p.sig_val
                        if op.dma and op.prev_ring is not None:
                            p = op.prev_ring
                            if need.get(p.sem_key, 0) < p.sig_val:
                                need[p.sem_key] = p.sig_val
                        for k, v in need.items():
                            if waited.get(k, 0) < v:
                                eng.wait_ge(sems[k], v)
                                waited[k] = v
                        ins = op.fn(eng)
                        if op.dma:
                            ins.then_inc(sems[op.sem_key], 16)
                        elif op.signal:
                            ins.then_inc(sems[op.sem_key], 1)

                return body

            block.tensor(run("pe"))
            block.scalar(run("act"))
            block.vector(run("dve"))
            block.gpsimd(run("pool"))
            block.sync(run("sp"))


def pipeline(n_items, stages, after_step=None):
    ns = len(stages)
    for step in range(n_items + ns - 1):
        for si in reversed(range(ns)):
            i = step - si
            if 0 <= i < n_items:
                stages[si](i)
        if after_step is not None:
            after_step(step)


class Arena:
    def __init__(self, nc, start, end):
        self.nc = nc
        self.p = start
        self.end = end
        self.n = 0

    def alloc(self, name, shape, dtype):
        esz = 2 if dtype == BF16 else 4
        size = esz
        for s in shape[1:]:
            size *= s
        size = (size + 31) // 32 * 32
        assert self.p + size <= self.end, (name, self.p, size, self.end)
        self.n += 1
        t = self.nc.alloc_sbuf_tensor_at("%s_%d" % (name, self.n), list(shape), dtype, offset=self.p)
        self.p += size
        return t

    def alloc_at(self, name, shape, dtype, offset):
        self.n += 1
        return self.nc.alloc_sbuf_tensor_at("%s_%d" % (name, self.n), list(shape), dtype, offset=offset)


def _bucket_table():
    oh = np.zeros((33, 384), np.float32)
    for m in range(383):
        n = m - 127
        if n < 0:
            oh[32, m] = NEG
            continue
        if n < 16:
            b = n
        else:
            nf = np.float32(max(n, 1))
            v = np.log(nf / np.float32(16)) / np.float32(math.log(128 / 16)) * np.float32(16)
            b = min(16 + int(np.float32(v)), 31)
        oh[b, m] = 1.0
    return oh


def host_prep(inputs, T, D):
    KC = D // 128
    f = lambda a: np.ascontiguousarray(np.asarray(a, dtype=np.float32))
    shared = {
        "w_in": f(inputs["w_in"][0]),
        "lambda_qk": f(inputs["lambda_qk"][0].reshape(1, 256)),
        "subln_w": f(inputs["subln_w"][0].reshape(1, 128)),
        "w_attn_proj": f(inputs["w_attn_proj"][0]),
        "conv_w": f(inputs["conv_w"][0].T.reshape(4, 128, CONV_K).transpose(1, 0, 2)),
        "conv_b": f(inputs["conv_b"][0].reshape(4, 128).T),
        "conv_ln_g": f(inputs["conv_ln_g"][0].reshape(4, 128).T),
        "conv_ln_b": f(inputs["conv_ln_b"][0].reshape(4, 128).T),
        "w_conv_proj": f(inputs["w_conv_proj"][0]),
        "b_conv_proj": f(inputs["b_conv_proj"][0].reshape(KC, 128).T),
        "w_out": f(inputs["w_out"][0]),
        "post_ln_g": f(inputs["post_ln_g"][0].reshape(1, D)),
        "post_ln_b": f(inputs["post_ln_b"][0].reshape(1, D)),
        "rel_bias": f(inputs["rel_bias"]),
        "onehot": _bucket_table(),
    }
    x = np.asarray(inputs["x"], dtype=np.float32)
    maps = []
    for b in range(x.shape[0]):
        m = dict(shared)
        m["x"] = np.ascontiguousarray(x[b])
        maps.append(m)
    return maps


def build(T=SEQ, D=D_MODEL):
    KC = D // 128
    NTB = T // 128
    NTT = T // 512
    NH2 = D // 512
    DIN = 3584 + 2 * D
    C_Q, C_K, C_V, C_GA, C_A, C_B, C_GC, C_G1, C_G2 = 0, 512, 1024, 1536, 2048, 2560, 3072, 3584, 3584 + D
    SCALE = 64 ** -0.5

    nc = bass.Bass("TRN2", target_bir_lowering=False)
    dt_in = lambda name, shape: nc.dram_tensor(name, list(shape), F32, kind="ExternalInput").ap()
    x_d = dt_in("x", [T, D])
    w_in_d = dt_in("w_in", [D, DIN])
    lam_d = dt_in("lambda_qk", [1, 256])
    subln_d = dt_in("subln_w", [1, 128])
    wap_d = dt_in("w_attn_proj", [512, D])
    convw_d = dt_in("conv_w", [128, 4, CONV_K])
    convb_d = dt_in("conv_b", [128, 4])
    clng_d = dt_in("conv_ln_g", [128, 4])
    clnb_d = dt_in("conv_ln_b", [128, 4])
    wcp_d = dt_in("w_conv_proj", [512, D])
    bcp_d = dt_in("b_conv_proj", [128, KC])
    wout_d = dt_in("w_out", [D, D])
    plg_d = dt_in("post_ln_g", [1, D])
    plb_d = dt_in("post_ln_b", [1, D])
    relb_d = dt_in("rel_bias", [32, 4])
    oh_d = dt_in("onehot", [33, 384])
    y_d = nc.dram_tensor("y", [T, D], F32, kind="ExternalOutput").ap()
    gscr = nc.dram_tensor("gscr", [4 * 128, 384], F32)

    S = Sched(nc)
    A = Arena(nc, 20 * 1024, 224 * 1024)

    pdd = [nc.alloc_psum_tensor("pd%d" % i, [128, 1024], F32) for i in range(4)]
    ps = [pdd[k // 2][:, (k % 2) * 512:(k % 2) * 512 + 512] for k in range(8)]
    pd = [pdd[2], pdd[3]]
    psb = [p.bitcast(BF16) for p in ps]

    def B(i):
        return ("ps", i)

    xT = A.alloc("xT", [128, KC, T], BF16)
    wbuf = [A.alloc("wbuf", [128, KC, 512], BF16) for _ in range(2)]
    oaT = A.alloc("oaT", [128, 4, T], BF16)
    ucT = A.alloc("ucT", [128, 4, T], BF16)
    ident = A.alloc("ident", [128, 128], BF16)
    identf = A.alloc("identf", [128, 128], F32)
    onesdiv = A.alloc("onesdiv", [128, 128], BF16)
    lng_bc = A.alloc("lng", [128, D], F32)
    lnb_bc = A.alloc("lnb", [128, D], F32)
    subw8 = A.alloc("subw8", [128, 128], F32)
    lamt = A.alloc("lamt", [128, 256], F32)
    lamp = A.alloc("lamp", [128, 128], F32)
    lams = A.alloc("lams", [128, 8], F32)
    neglam = A.alloc("neglam", [128, 1], F32)
    mhalf = A.alloc("mhalf", [128, 8], F32)
    epsb = A.alloc("epsb", [128, 1], F32)
    convw = A.alloc("convw", [128, 4, CONV_K], F32)
    convb = A.alloc("convb", [128, 4], F32)
    clng = A.alloc("clng", [128, 4], F32)
    clnb = A.alloc("clnb", [128, 4], F32)
    bcp = A.alloc("bcp", [128, KC], F32)
    ovl = A.p

    S.add("dve", lambda e: e.memset(identf[:, :], 1.0), writes=["identf"])
    S.add("pool", lambda e: e.affine_select(out=identf[:, :], in_=identf[:, :], compare_op=ALU.is_equal, fill=0.0,
                                            base=0, pattern=[[-1, 128]], channel_multiplier=1),
          reads=["identf"], writes=["identf"])
    S.add("dve", lambda e: e.tensor_copy(ident[:, :], identf[:, :]), reads=["identf"], writes=["ident"])
    S.add("dve", lambda e: e.memset(onesdiv[:, :], 1.0 / 512.0), writes=["onesdiv"])
    S.add("dve", lambda e: e.memset(mhalf[:, :], -0.5), writes=["mhalf"])
    S.add("dve", lambda e: e.memset(epsb[:, :], LN_EPS), writes=["epsb"])
    small = [(lng_bc, plg_d.partition_broadcast(128), "lng"), (lnb_bc, plb_d.partition_broadcast(128), "lnb"),
             (subw8, subln_d.partition_broadcast(128), "subw8"), (lamt, lam_d.partition_broadcast(128), "lamt"),
             (convw, convw_d, "convw"), (convb, convb_d, "convb"), (clng, clng_d, "clng"), (clnb, clnb_d, "clnb"),
             (bcp, bcp_d, "bcp")]
    for tt, src, key in small:
        if len(tt.shape) == 3:
            S.add("sp", (lambda tt=tt, src=src: lambda e: e.dma_start(out=tt[:, :, :], in_=src[:, :, :]))(), writes=[key], dma=True)
        else:
            S.add("sp", (lambda tt=tt, src=src: lambda e: e.dma_start(out=tt[:, :], in_=src))(), writes=[key], dma=True)
    S.add("dve", lambda e: e.tensor_scalar(out=subw8[:, :], in0=subw8[:, :], scalar1=1.0 - LAMBDA_INIT, scalar2=None,
                                           op0=ALU.mult), reads=["subw8"], writes=["subw8"])
    S.add("dve", lambda e: e.tensor_tensor(out=lamp[:, 0:64], in0=lamt[:, 0:64], in1=lamt[:, 64:128], op=ALU.mult),
          reads=["lamt"], writes=["lamp0"])
    S.add("dve", lambda e: e.tensor_tensor(out=lamp[:, 64:128], in0=lamt[:, 128:192], in1=lamt[:, 192:256], op=ALU.mult),
          reads=["lamt"], writes=["lamp1"])
    S.add("dve", lambda e: e.reduce_sum(out=lams[:, 0:1], in_=lamp[:, 0:64], axis=mybir.AxisListType.X),
          reads=["lamp0"], writes=["lams0"])
    S.add("dve", lambda e: e.reduce_sum(out=lams[:, 1:2], in_=lamp[:, 64:128], axis=mybir.AxisListType.X),
          reads=["lamp1"], writes=["lams1"])
    S.add("act", lambda e: e.activation(out=lams[:, 2:4], in_=lams[:, 0:2], func=AF.Exp), reads=["lams0", "lams1"],
          writes=["lams2"])
    S.add("dve", lambda e: e.scalar_tensor_tensor(out=neglam[:, :], in0=lams[:, 3:4], scalar=-LAMBDA_INIT, in1=lams[:, 2:3],
                                                  op0=ALU.add, op1=ALU.subtract), reads=["lams2"], writes=["neglam"])

    wcount = [0]

    deferred_w = []

    def load_w(c0, ncols=512, split=1):
        i = wcount[0] % 2
        wcount[0] += 1
        step = ncols // split
        for k in range(split):
            src = w_in_d[:, c0 + k * step:c0 + (k + 1) * step].rearrange("(c p) n -> p c n", p=128)

            def issue(src=src, k=k):
                S.add("pool", lambda e: e.dma_start(out=wbuf[i][:, :, k * step:(k + 1) * step], in_=src),
                      writes=[("w", i, k)] if split > 1 else [("w", i)] + [("w", i, kk) for kk in range(4)], dma=True)
            if k == 0:
                issue()
            else:
                deferred_w.append(issue)
        return i

    evac_rr = [0]

    def evac_engine():
        evac_rr[0] += 1
        return "act" if evac_rr[0] % 2 == 0 else "dve"

    def copy_on(eng, out, in_):
        if eng == "act":
            return lambda e: e.copy(out=out, in_=in_)
        return lambda e: e.tensor_copy(out, in_)

    markA = A.p
    qT = A.alloc("qT", [128, 4, T], BF16)
    kT = A.alloc("kT", [128, 4, T], BF16)
    vaug = A.alloc("vaug", [128, NTB, 4, 130], BF16)
    sga = A.alloc("sga", [128, NTB, 512], BF16)
    NPT = 5
    pT2 = [A.alloc("pT", [128, 2, 512], BF16) for _ in range(NPT)]
    BT8 = A.alloc("BT8", [128, 4, 2, 128], BF16)
    btf = [A.alloc("btf", [128, 128], F32) for _ in range(2)]
    NXS = 4
    xbst = [A.alloc("xbst", [128, D], BF16) for _ in range(NXS)]
    relb = A.alloc("relb", [33, 4], F32)
    ohs = A.alloc("ohs", [33, 384], F32)
    lhsg = [A.alloc("lhsg", [33, 128], F32) for _ in range(4)]
    grep = [A.alloc("grep", [128, 384], F32) for _ in range(2)]
    NOB = 2
    osb = [A.alloc("osb", [128, 8, 130], F32) for _ in range(NOB)]
    rs = [A.alloc("rs", [128, 8], F32) for _ in range(NOB)]
    od = [A.alloc("od", [128, 4, 128], F32) for _ in range(NOB)]
    osq = [A.alloc("osq", [128, 128], F32) for _ in range(4)]
    osqc = [0]
    ssq = [A.alloc("ssq", [128, 4], F32) for _ in range(NOB)]
    rstd = [A.alloc("rstd", [128, 4], F32) for _ in range(NOB)]
    NOAB = 4
    oab = [A.alloc("oab", [128, 4, 128], BF16) for _ in range(NOAB)]
    tcount = [0]

    for n in range(NTB):
        S.add("dve", lambda e, n=n: e.memset(vaug[:, n, :, 128:130], 1.0), writes=[("vones", n)])

    xT_all = [("xT", n) for n in range(NTB)]

    pcount = [0]

    def proj_fm(wi, ncols, sink, tiles=None, wsplit=False):
        order = [(mi, t) for mi in range(ncols // 128) for t in range(NTT)] if tiles is None else \
                [(mi, t) for t in tiles for mi in range(ncols // 128)]
        for (mi, t) in order:
            if True:
                bank = 4 + pcount[0] % 4
                pcount[0] += 1

                def mm(e, wi=wi, mi=mi, t=t, bank=bank):
                    ins = None
                    for c in range(KC):
                        ins = e.matmul(ps[bank][:, :], wbuf[wi][:, c, mi * 128:(mi + 1) * 128], xT[:, c, t * 512:(t + 1) * 512],
                                       start=(c == 0), stop=(c == KC - 1))
                    return ins
                S.add("pe", mm, reads=[("w", wi, mi) if wsplit else ("w", wi)] + xT_all[t * 4:(t + 1) * 4], banks=[B(bank)])
                sink(mi, t, bank)

    def proj_tm(wi, sink):
        for n in range(NTB):
            bank = 4 + pcount[0] % 4
            pcount[0] += 1

            def mm(e, wi=wi, n=n, bank=bank):
                ins = None
                for c in range(KC):
                    ins = e.matmul(ps[bank][:, :], xT[:, c, n * 128:(n + 1) * 128], wbuf[wi][:, c, :],
                                   start=(c == 0), stop=(c == KC - 1))
                return ins
            S.add("pe", mm, reads=[("w", wi), ("xT", n)], banks=[B(bank)])
            sink(n, bank)

    def load_x_block(n):
        if True:
            st = xbst[n % NXS]
            S.add("pool", (lambda st=st, n=n: lambda e: e.dma_start(out=st[:, :], in_=x_d[n * 128:(n + 1) * 128, :]))(),
                  writes=[("xbst", n % NXS)], dma=True)
            bank = n % 2
            for half in range((KC + 7) // 8):
                c0 = half * 8
                cn = min(8, KC - c0)

                def tr(e, st=st, bank=bank, c0=c0, cn=cn):
                    ins = None
                    for c in range(cn):
                        ins = e.transpose(psb[bank][:, c * 128:(c + 1) * 128], st[:, (c0 + c) * 128:(c0 + c + 1) * 128], ident[:, :])
                    return ins
                S.add("pe", tr, reads=[("xbst", n % NXS), "ident"], banks=[B(bank)])
                eng = evac_engine()
                outap = xT[:, c0:c0 + cn, n * 128:(n + 1) * 128]
                inap = psb[bank][:, 0:cn * 128].rearrange("p (c t) -> p c t", c=cn)
                S.add(eng, copy_on(eng, outap, inap), writes=[("xT", n)], banks=[B(bank)])


    wi_q = load_w(C_Q, split=4)

    def sink_q(mi, t, bank):
        eng = evac_engine()
        S.add(eng, copy_on(eng, qT[:, mi, t * 512:(t + 1) * 512], ps[bank][:, :]), writes=[("qT", mi, t)], banks=[B(bank)])
    S.add("sp", lambda e: e.dma_start(out=relb[0:32, :], in_=relb_d[:, :]), writes=["relb"], dma=True)
    S.add("dve", lambda e: e.memset(relb[32:33, :], 1.0), writes=["relb1"])
    S.add("sp", lambda e: e.dma_start(out=ohs[:, :], in_=oh_d[:, :]), writes=["ohs"], dma=True)
    for h in range(4):
        S.add("dve", (lambda h=h: lambda e: e.tensor_copy(lhsg[h][:, :], relb[:, h:h + 1].to_broadcast([33, 128])))(),
              reads=["relb", "relb1"], writes=[("lhsg", h)])

    bt8_pending = []

    def flush_bt8():
        while bt8_pending:
            bt8_pending.pop(0)()

    def g_step(h):
        gb = 2 + h % 2
        flush_bt8()
        S.add("pe", lambda e: e.matmul(ps[gb][:, 0:384], lhsg[h][:, :], ohs[:, :], start=True, stop=True),
              reads=[("lhsg", h), "ohs"], banks=[B(gb)])
        S.add("dve", lambda e: e.tensor_copy(grep[h % 2][:, 383:384], ps[gb][:, 382:383]), writes=[("gfar", h % 2)], banks=[B(gb)])
        S.add("dve", lambda e: e.tensor_scalar(out=grep[h % 2][:, 0:383], in0=ps[gb][:, 0:383], scalar1=grep[h % 2][:, 383:384], scalar2=None,
                                               op0=ALU.subtract), reads=[("gfar", h % 2)], writes=[("grep", h % 2)], banks=[B(gb)])
        S.add("sp", lambda e: e.dma_start(out=gscr.ap()[h * 128:(h + 1) * 128, :], in_=grep[h % 2][:, :]),
              reads=[("grep", h % 2), ("gfar", h % 2)], writes=[("gscr", h)], dma=True)
        for dist in range(2):
            src = AP(gscr, h * 128 * 384 + dist * 128 + 127, [[383, 128], [1, 128]])
            S.add("sp", (lambda src=src, dist=dist: lambda e: e.dma_start(out=btf[dist][:, :], in_=src))(), reads=[("gscr", h)],
                  writes=[("btf", dist)], dma=True)
            bt8_pending.append((lambda dist=dist: lambda: S.add(
                "dve", lambda e: e.tensor_scalar(out=BT8[:, h, dist, :], in0=btf[dist][:, :], scalar1=8.0, scalar2=None, op0=ALU.mult),
                reads=[("btf", dist)], writes=[("BT8", h, dist)]))())

    def sink_k(mi, t, bank):
        eng = evac_engine()
        S.add(eng, copy_on(eng, kT[:, mi, t * 512:(t + 1) * 512], ps[bank][:, :]), writes=[("kT", mi, t)], banks=[B(bank)])
    wi_k = None
    for n in range(4):
        load_x_block(n)
    while deferred_w:
        deferred_w.pop(0)()
    wi_k = load_w(C_K)
    for t in range(NTT):
        proj_fm(wi_q, 512, sink_q, tiles=[t], wsplit=True)
        if t + 1 < NTT:
            for n in range(4 * (t + 1), 4 * (t + 1) + 4):
                load_x_block(n)
        proj_fm(wi_k, 512, sink_k, tiles=[t])
        if t < 4:
            g_step(t)
    for h in range(NTT, 4):
        g_step(h)
    wi_v = load_w(C_V)
    wi_ga = load_w(C_GA)

    def sink_v(n, bank):
        eng = evac_engine()
        S.add(eng, copy_on(eng, vaug[:, n, :, 0:128], ps[bank][:, :].rearrange("p (h e) -> p h e", h=4)),
              writes=[("v", n)], banks=[B(bank)])
    proj_tm(wi_v, sink_v)

    def sink_ga(n, bank):
        S.add("act", lambda e, n=n, bank=bank: e.activation(out=sga[:, n, :], in_=ps[bank][:, :], func=AF.Silu),
              writes=[("sga", n)], banks=[B(bank)])
    proj_tm(wi_ga, sink_ga)
    flush_bt8()
    wi_b = load_w(C_B)
    wi_a = load_w(C_A)

    if NTT == 4:
        torder = {0: (3, 0, 2, 1), 1: (3, 0, 2, 1), 2: (3, 0, 2, 1), 3: (3, 0, 2, 1)}
    else:
        torder = {h: tuple(reversed(range(NTT))) for h in range(4)}
    steps = [(h, t, j) for h in range(4) for t in torder[h] for j in range(4 * t + 4)]
    LOOK = 2
    ptc = [0, 0]
    info = {}
    pending = []
    deferred_tr = []
    cur_step = [0]
    ACCB = [0, 1, 2]
    TRB = 3
    NWARM = 2

    def acc_loc(c, rr):
        a = c * 4 + rr
        return ACCB[a // 3], (a % 3) * 130, (a % 3 == 0)

    def emit_scores(i):
        h, t, j = steps[i]
        r = j - 4 * t if j >= 4 * t else 0
        c0 = r * 128
        ncol = 512 - c0
        sb = [4 + (i % 2) * 2, 5 + (i % 2) * 2]

        def sc(e):
            ins = None
            for c in range(2):
                pr = slice(c * 64, (c + 1) * 64)
                ins = e.matmul(ps[sb[c]][:, 0:ncol], kT[pr, h, j * 128:(j + 1) * 128],
                               qT[pr, h, t * 512 + c0:(t + 1) * 512], start=True, stop=True)
            for c in range(2):
                for dist in range(2):
                    qb = j + dist
                    if qb < 4 * t or qb >= 4 * t + 4:
                        continue
                    off = (qb - 4 * t) * 128 - c0
                    ins = e.matmul(ps[sb[c]][:, off:off + 128], ident[:, :], BT8[:, h, dist, :], start=False, stop=True,
                                   skip_group_check=True)
            return ins
        S.add("pe", sc, reads=[("kT", h, j // 4), ("qT", h, t), ("BT8", h, 0), ("BT8", h, 1), "ident"],
              banks=[B(sb[0]), B(sb[1])])
        pi = ptc[0] % NPT
        ptc[0] += 1
        pts = [pi, pi]
        pdv = pd[i % 2][:, :].rearrange("p (c n) -> p c n", c=2)
        S.add("act", lambda e: e.activation(out=pT2[pi][:, :, 0:ncol], in_=pdv[:, :, 0:ncol], func=AF.Exp, scale=SCALE),
              writes=[("pT", 0, pi), ("pT", 1, pi)], banks=[B(sb[0]), B(sb[1])])
        info[i] = (r, c0, pts)

    def emit_pv(i):
        h, t, j = steps[i]
        r, c0, pts = info.pop(i)

        def pv(e):
            ins = None
            for c in range(2):
                for rr in range(r, 4):
                    bank, col, first = acc_loc(c, rr)
                    ins = e.matmul(ps[bank][:, col:col + 130], pT2[pts[c]][:, c, rr * 128 - c0:(rr + 1) * 128 - c0],
                                   vaug[:, j, h, :], start=(j == 0 and first), stop=(j == 4 * t + rr), skip_group_check=True)
            return ins
        if j == 0 and i > 0:
            def warm(e):
                ins = None
                for _ in range(NWARM):
                    ins = e.matmul(ps[TRB][:, :], ident[:, :], qT[:, 0, 0:512], start=True, stop=True)
                return ins
            S.add("pe", warm, reads=["ident", ("qT", 0, 0)], banks=[B(TRB)])
        S.add("pe", pv, reads=[("pT", 0, pts[0]), ("pT", 1, pts[1]), ("v", j), ("vones", j)], banks=[B(b) for b in ACCB])
        if j == 4 * t + 3:
            tile_epilogue(h, t)

    def tile_epilogue(h, t):
        ob = tcount[0] % NOB
        tb = tcount[0] % NOAB
        tcount[0] += 1
        osb_ = osb[ob]
        for bi, bank in enumerate(ACCB):
            na = 3 if bi < 2 else 2
            eng = "act"
            S.add(eng, copy_on(eng, osb_[:, 3 * bi:3 * bi + na, :], ps[bank][:, 0:na * 130].rearrange("p (a e) -> p a e", a=na)),
                  writes=[("o", ob, bi)], banks=[B(bank)])
        okeys = [("o", ob, bi) for bi in range(3)]
        chunks = []

        def c_rs():
            S.add("dve", lambda e: e.reciprocal(out=rs[ob][:, 0:8], in_=osb_[:, :, 128]), reads=okeys, writes=[("rs", ob)])
            S.add("dve", lambda e: e.tensor_scalar(out=rs[ob][:, 4:8], in0=rs[ob][:, 4:8], scalar1=neglam[:, 0:1], scalar2=None,
                                                   op0=ALU.mult), reads=[("rs", ob), "neglam"], writes=[("rs", ob)])
        chunks.append(c_rs)
        for rr in range(4):
            def c_od(rr=rr):
                S.add("dve", lambda e: e.tensor_scalar(out=od[ob][:, rr, :], in0=osb_[:, rr, 0:128], scalar1=rs[ob][:, rr:rr + 1],
                                                       scalar2=None, op0=ALU.mult),
                      reads=[("rs", ob)] + okeys, writes=[("od", ob, rr)])
                S.add("dve", lambda e: e.scalar_tensor_tensor(out=od[ob][:, rr, :], in0=osb_[:, 4 + rr, 0:128],
                                                              scalar=rs[ob][:, 4 + rr:5 + rr], in1=od[ob][:, rr, :],
                                                              op0=ALU.mult, op1=ALU.add),
                      reads=[("rs", ob), ("od", ob, rr)] + okeys, writes=[("od", ob, rr)])
                oq = osqc[0] % 4
                osqc[0] += 1
                S.add("dve", lambda e: e.scalar_tensor_tensor(out=osq[oq][:, :], in0=od[ob][:, rr, :], scalar=1.0, in1=od[ob][:, rr, :],
                                                              op0=ALU.mult, op1=ALU.mult, accum_out=ssq[ob][:, rr:rr + 1]),
                      reads=[("od", ob, rr)], writes=[("osq", oq), ("ssq", ob, rr)])
            chunks.append(c_od)

        def c_rstd():
            S.add("dve", lambda e: e.tensor_scalar(out=rstd[ob][:, :], in0=ssq[ob][:, :], scalar1=1.0 / 128.0, scalar2=LN_EPS,
                                                   op0=ALU.mult, op1=ALU.add),
                  reads=[("ssq", ob, rr) for rr in range(4)], writes=[("rstd", ob)])
            S.add("pool", lambda e: e.tensor_tensor(out=rstd[ob][:, :], in0=rstd[ob][:, :], in1=mhalf[:, 0:4], op=ALU.pow),
                  reads=[("rstd", ob), "mhalf"], writes=[("rstd", ob)])
        chunks.append(c_rstd)
        chunks.extend([None, None, None, None])
        for rr in range(4):
            def c_fin(rr=rr):
                n = 4 * t + rr
                S.add("dve", lambda e: e.scalar_tensor_tensor(out=od[ob][:, rr, :], in0=od[ob][:, rr, :], scalar=rstd[ob][:, rr:rr + 1],
                                                              in1=subw8[:, :], op0=ALU.mult, op1=ALU.mult),
                      reads=[("rstd", ob), ("od", ob, rr), "subw8"], writes=[("od", ob, rr)])
                S.add("pool", lambda e: e.tensor_tensor(out=oab[tb][:, rr, :], in0=od[ob][:, rr, :],
                                                        in1=sga[:, n, h * 128:(h + 1) * 128], op=ALU.mult),
                      reads=[("od", ob, rr), ("sga", n)], writes=[("oab", tb, rr)])
            chunks.append(c_fin)

        def c_tr():
            def tro(e):
                ins = None
                for rr in range(4):
                    ins = e.transpose(psb[TRB][:, rr * 128:(rr + 1) * 128], oab[tb][:, rr, :], ident[:, :])
                return ins
            S.add("pe", tro, reads=[("oab", tb, rr) for rr in range(4)] + ["ident"], banks=[B(TRB)])
            S.add("dve", copy_on("dve", oaT[:, h, t * 512:(t + 1) * 512], psb[TRB][:, 0:512]), writes=[("oaT", h, t)], banks=[B(TRB)])
        deferred_tr.append((cur_step[0] + 16, c_tr))
        pending.append(chunks)

    def drain(k):
        while k > 0 and pending:
            ch = pending[0]
            f = ch.pop(0)
            if f is not None:
                f()
            k -= 1
            if not ch:
                pending.pop(0)

    for i in range(len(steps) + LOOK):
        cur_step[0] = i
        if i < len(steps):
            emit_scores(i)
        if i >= LOOK:
            while deferred_tr and deferred_tr[0][0] <= i:
                deferred_tr.pop(0)[1]()
            emit_pv(i - LOOK)
            drain(6)
    drain(10 ** 6)
    upad = A.alloc_at("upad", [128, 4, T + 32], BF16, markA)
    qk_keys = [("qT", h, t) for h in range(4) for t in range(NTT)] + [("kT", h, t) for h in range(4) for t in range(NTT)]
    S.add("dve", lambda e: e.memset(upad[:, :, 0:30], 0.0), writes=["upad0"], after=qk_keys)
    diag_off = markA + ((4 * (T + 32) * 2 + 31) // 32) * 32
    diag = A.alloc_at("diag", [128, 4, CONV_K, 128], BF16, diag_off)
    diag_thunks = []
    diag_over = qk_keys + [("v", n) for n in range(NTB)] + [("vones", n) for n in range(NTB)]
    for cc in range(4):
        for j in range(CONV_K):
            if (cc * CONV_K + j) % 2 == 0:
                diag_thunks.append(lambda cc=cc, j=j: S.add(
                    "dve", lambda e: e.tensor_scalar(out=diag[:, cc, j, :], in0=identf[:, :], scalar1=convw[:, cc, j:j + 1],
                                                     scalar2=None, op0=ALU.mult),
                    reads=["identf", "convw"], writes=[("diag", cc, j)], after=diag_over))
            else:
                diag_thunks.append(lambda cc=cc, j=j: S.add(
                    "act", lambda e: e.activation(out=diag[:, cc, j, :], in_=identf[:, :], func=AF.Copy, scale=convw[:, cc, j:j + 1]),
                    reads=["identf", "convw"], writes=[("diag", cc, j)], after=diag_over))

    def sink_bb(mi, t, bank):
        S.add("act", lambda e, mi=mi, t=t, bank=bank: e.activation(out=upad[:, mi, 30 + t * 512:30 + (t + 1) * 512], in_=ps[bank][:, :],
                                                                   func=AF.Sigmoid),
              writes=[("usig", mi, t)], banks=[B(bank)], after=qk_keys)
        for _ in range(4):
            if diag_thunks:
                diag_thunks.pop(0)()
    proj_fm(wi_b, 512, sink_bb)

    def sink_a(mi, t, bank):
        S.add("dve", lambda e, mi=mi, t=t, bank=bank: e.tensor_tensor(out=upad[:, mi, 30 + t * 512:30 + (t + 1) * 512],
                                                                     in0=ps[bank][:, :], in1=upad[:, mi, 30 + t * 512:30 + (t + 1) * 512],
                                                                     op=ALU.mult),
              reads=[("usig", mi, t)], writes=[("u", mi, t)], banks=[B(bank)])
        for _ in range(4):
            if diag_thunks:
                diag_thunks.pop(0)()
    proj_fm(wi_a, 512, sink_a)

    while diag_thunks:
        diag_thunks.pop(0)()
    wi_gc = load_w(C_GC)
    while deferred_tr:
        deferred_tr.pop(0)[1]()

    S.soft_barrier()
    A.p = markA + ((4 * (T + 32) * 2 + 31) // 32) * 32

    assert A.p == diag_off
    A.p += 4 * CONV_K * 128 * 2
    ycf = [A.alloc("ycf", [128, 4, 512], F32) for _ in range(2)]
    ybf = [A.alloc("ybf", [128, 4, 512], BF16) for _ in range(2)]
    ysq = [A.alloc("ysq", [128, 4, 512], BF16) for _ in range(2)]
    mean_sb = [A.alloc("mean_sb", [128, 512], F32) for _ in range(2)]
    var_sb = [A.alloc("var_sb", [128, 512], F32) for _ in range(2)]
    rstd_c = [A.alloc("rstd_c", [128, 512], F32) for _ in range(2)]
    zt = [A.alloc("zt", [128, 512], F32) for _ in range(2)]
    wap = A.alloc("wap", [128, 4, D], BF16)
    wcp = A.alloc("wcp", [128, 4, D], BF16)
    endB = A.p
    S.add("pool", lambda e: e.dma_start(out=wap[:, :, :], in_=wap_d.rearrange("(c p) n -> p c n", p=128)), writes=["wap"], dma=True)
    S.add("pool", lambda e: e.dma_start(out=wcp[:, :, :], in_=wcp_d.rearrange("(c p) n -> p c n", p=128)), writes=["wcp"], dma=True)

    def load_wg(m):
        i = wcount[0] % 2
        wcount[0] += 1
        s1 = w_in_d[:, C_G1 + m * 128:C_G1 + (m + 1) * 128].rearrange("(c p) n -> p c n", p=128)
        s2 = w_in_d[:, C_G2 + m * 128:C_G2 + (m + 1) * 128].rearrange("(c p) n -> p c n", p=128)
        prev = [("w", i)] + [("w", i, kk) for kk in range(4)] + [("wgA", i), ("wgB", i)]
        S.add("pool", lambda e: e.dma_start(out=wbuf[i][:, :, 0:128], in_=s1), writes=[("wgA", i)], dma=True, after=prev)
        S.add("pool", lambda e: e.dma_start(out=wbuf[i][:, :, 128:256], in_=s2), writes=[("wgB", i)], dma=True, after=prev)
        return i

    wg_next = load_wg(0)

    NDT = 3

    def st0(t, inject=None):
        sl = t % 2
        for cc in range(4):
            bank = cc
            if cc == 1 and inject is not None:
                inject()

            urd = ["upad0", ("u", cc, t)] + ([("u", cc, t - 1)] if t > 0 else [])
            for j in range(NDT):
                if j == 0:
                    S.add("dve", lambda e, cc=cc: e.tensor_scalar(out=ycf[sl][:, cc, :], in0=upad[:, cc, t * 512:t * 512 + 512],
                                                                  scalar1=convw[:, cc, 0:1], scalar2=None, op0=ALU.mult),
                          reads=urd + ["convw"], writes=[("ycf", sl, cc)])
                else:
                    S.add("dve", lambda e, cc=cc, j=j: e.scalar_tensor_tensor(
                        out=ycf[sl][:, cc, :], in0=upad[:, cc, t * 512 + j:t * 512 + j + 512], scalar=convw[:, cc, j:j + 1],
                        in1=ycf[sl][:, cc, :], op0=ALU.mult, op1=ALU.add),
                        reads=urd + ["convw", ("ycf", sl, cc)], writes=[("ycf", sl, cc)])

            def cv(e, cc=cc, bank=bank):
                ins = None
                for j in range(NDT, CONV_K):
                    ins = e.matmul(ps[bank][:, :], diag[:, cc, j, :], upad[:, cc, t * 512 + j:t * 512 + j + 512],
                                   start=(j == NDT), stop=(j == CONV_K - 1))
                return ins
            rd = [("diag", cc, j) for j in range(NDT, CONV_K)] + urd
            S.add("pe", cv, reads=rd, banks=[B(bank)])
            S.add("dve", lambda e, cc=cc, bank=bank: e.scalar_tensor_tensor(out=ycf[sl][:, cc, :], in0=ps[bank][:, :],
                                                                            scalar=convb[:, cc:cc + 1], in1=ycf[sl][:, cc, :],
                                                                            op0=ALU.add, op1=ALU.add),
                  reads=["convb", ("ycf", sl, cc)], writes=[("ycf", sl, cc)], banks=[B(bank)])
            S.add("act", lambda e, cc=cc: e.copy(out=ybf[sl][:, cc, :], in_=ycf[sl][:, cc, :]), reads=[("ycf", sl, cc)],
                  writes=[("ybf", sl, cc)])
            S.add("act", lambda e, cc=cc: e.activation(out=ysq[sl][:, cc, :], in_=ycf[sl][:, cc, :], func=AF.Square),
                  reads=[("ycf", sl, cc)], writes=[("ysq", sl, cc)])

    def st1a(t):
        sl = t % 2
        b4, b5 = 4 + 2 * sl, 5 + 2 * sl

        def stats(e):
            ins = None
            for cc in range(4):
                ins = e.matmul(ps[b4][:, :], onesdiv[:, :], ybf[sl][:, cc, :], start=(cc == 0), stop=(cc == 3))
            for cc in range(4):
                ins = e.matmul(ps[b5][:, :], onesdiv[:, :], ysq[sl][:, cc, :], start=(cc == 0), stop=(cc == 3))
            return ins
        S.add("pe", stats, reads=[("ybf", sl, cc) for cc in range(4)] + [("ysq", sl, cc) for cc in range(4)] + ["onesdiv"],
              banks=[B(b4), B(b5)])

    def st1b(t):
        sl = t % 2
        b4, b5 = 4 + 2 * sl, 5 + 2 * sl
        S.add("act", lambda e: e.copy(out=mean_sb[sl][:, :], in_=ps[b4][:, :]), writes=[("mean_sb", sl)], banks=[B(b4)])
        S.add("dve", lambda e: e.tensor_tensor(out=var_sb[sl][:, :], in0=mean_sb[sl][:, :], in1=mean_sb[sl][:, :], op=ALU.mult),
              reads=[("mean_sb", sl)], writes=[("var_sb", sl)])
        S.add("dve", lambda e: e.scalar_tensor_tensor(out=var_sb[sl][:, :], in0=ps[b5][:, :], scalar=LN_EPS, in1=var_sb[sl][:, :],
                                                      op0=ALU.add, op1=ALU.subtract),
              reads=[("var_sb", sl)], writes=[("var_sb", sl)], banks=[B(b5)])
        S.add("act", lambda e: e.activation(out=var_sb[sl][:, :], in_=var_sb[sl][:, :], func=AF.Ln),
              reads=[("var_sb", sl)], writes=[("var_sb", sl)])
        S.add("act", lambda e: e.activation(out=rstd_c[sl][:, :], in_=var_sb[sl][:, :], func=AF.Exp, scale=-0.5),
              reads=[("var_sb", sl)], writes=[("rstd_c", sl)])

    def st2(t):
        sl = t % 2
        for cc in range(4):
            zi = cc % 2
            z = zt[zi]
            zk = ("zt", zi)
            S.add("dve", lambda e, cc=cc, z=z: e.tensor_tensor(out=z[:, :], in0=ycf[sl][:, cc, :], in1=mean_sb[sl][:, :], op=ALU.subtract),
                  reads=[("ycf", sl, cc), ("mean_sb", sl)], writes=[zk])
            S.add("dve", lambda e, z=z: e.tensor_tensor(out=z[:, :], in0=z[:, :], in1=rstd_c[sl][:, :], op=ALU.mult),
                  reads=[zk, ("rstd_c", sl)], writes=[zk])
            S.add("act", lambda e, cc=cc, z=z: e.activation(out=ucT[:, cc, t * 512:(t + 1) * 512], in_=z[:, :], func=AF.Silu,
                                                            scale=clng[:, cc:cc + 1], bias=clnb[:, cc:cc + 1]),
                  reads=[zk, "clng", "clnb"], writes=[("zs", cc, t)])

    gcc = [0]

    def sink_gc(mi, t, bank):
        gi = gcc[0] % 2
        gcc[0] += 1
        S.add("act", lambda e: e.activation(out=zt[gi][:, :], in_=ps[bank][:, :], func=AF.Silu), writes=[("zt", gi)], banks=[B(bank)])
        S.add("dve", lambda e: e.tensor_tensor(out=ucT[:, mi, t * 512:(t + 1) * 512], in0=ucT[:, mi, t * 512:(t + 1) * 512],
                                               in1=zt[gi][:, :], op=ALU.mult),
              reads=[("zt", gi), ("zs", mi, t)], writes=[("ucT", mi, t)])
    def gc_hook(step):
        t = step - 2
        if 0 <= t < NTT:
            proj_fm(wi_gc, 512, sink_gc, tiles=[t])
    for step in range(NTT + 2):
        if 0 <= step - 2 < NTT:
            st2(step - 2)
        if step < NTT:
            st0(step, inject=(lambda t=step - 1: st1a(t)) if step >= 1 else None)
        elif 0 <= step - 1 < NTT:
            st1a(step - 1)
        if 0 <= step - 1 < NTT:
            st1b(step - 1)
        gc_hook(step)

    S.soft_barrier()
    A.p = markA
    if _DBG.get("stop") == "B":
        S.emit()
        return nc

    mergedT = A.alloc("mergedT", [128, KC, T], BF16)
    wout = A.alloc("wout", [128, KC, D], BF16)
    markC2 = A.p
    sg1 = [A.alloc("sg1", [128, 512], F32) for _ in range(2)]
    sg2 = [A.alloc("sg2", [128, 512], F32) for _ in range(2)]
    m1 = [A.alloc("m1", [128, 512], F32) for _ in range(2)]
    m2 = [A.alloc("m2", [128, 512], F32) for _ in range(2)]

    assert A.p <= endB - 2 * 4 * D * 2, (A.p, endB)
    def load_wout():
        for c in range(KC):
            S.add("pool", lambda e, c=c: e.dma_start(out=wout[:, c, :], in_=wout_d[c * 128:(c + 1) * 128, :]), writes=[("wout", c)],
                  dma=True)

    it = 0
    for m in range(KC):
        wi = wg_next
        if m + 1 < KC:
            wg_next = load_wg(m + 1)
        if m == min(2, KC - 1):
            load_wout()
        for t in range(NTT):
            bk = [(it % 2) * 4 + i for i in range(4)]
            sl = it % 2
            it += 1
            tok = slice(t * 512, (t + 1) * 512)

            def mmC(e, wi=wi, m=m, tok=tok, bk=bk):
                ins = None
                for c in range(KC):
                    ins = e.matmul(ps[bk[0]][:, :], wbuf[wi][:, c, 0:128], xT[:, c, tok], start=(c == 0), stop=(c == KC - 1))
                for c in range(KC):
                    ins = e.matmul(ps[bk[1]][:, :], wbuf[wi][:, c, 128:256], xT[:, c, tok], start=(c == 0), stop=(c == KC - 1))
                for c in range(4):
                    ins = e.matmul(ps[bk[2]][:, :], wap[:, c, m * 128:(m + 1) * 128], oaT[:, c, tok], start=(c == 0), stop=(c == 3))
                for c in range(4):
                    ins = e.matmul(ps[bk[3]][:, :], wcp[:, c, m * 128:(m + 1) * 128], ucT[:, c, tok], start=(c == 0), stop=(c == 3))
                return ins
            S.add("pe", mmC, reads=[("wgA", wi), ("wgB", wi), "wap", "wcp"] + [("oaT", c, t) for c in range(4)] + [("ucT", c, t) for c in range(4)],
                  banks=[B(b) for b in bk])
            S.add("act", lambda e, sl=sl, bk=bk: e.activation(out=sg1[sl][:, :], in_=ps[bk[0]][:, :], func=AF.Sigmoid),
                  writes=[("sg1", sl)], banks=[B(bk[0])])
            S.add("act", lambda e, sl=sl, bk=bk: e.activation(out=sg2[sl][:, :], in_=ps[bk[1]][:, :], func=AF.Sigmoid),
                  writes=[("sg2", sl)], banks=[B(bk[1])])
            S.add("dve", lambda e, sl=sl, bk=bk: e.tensor_tensor(out=m1[sl][:, :], in0=ps[bk[2]][:, :], in1=sg1[sl][:, :], op=ALU.mult),
                  reads=[("sg1", sl)], writes=[("m1", sl)], banks=[B(bk[2])])
            S.add("dve", lambda e, sl=sl, bk=bk, m=m: e.scalar_tensor_tensor(out=m2[sl][:, :], in0=ps[bk[3]][:, :], scalar=bcp[:, m:m + 1],
                                                                             in1=sg2[sl][:, :], op0=ALU.add, op1=ALU.mult),
                  reads=[("sg2", sl), "bcp"], writes=[("m2", sl)], banks=[B(bk[3])])
            S.add("pool", lambda e, sl=sl, m=m, tok=tok: e.tensor_tensor(out=mergedT[:, m, tok], in0=m1[sl][:, :], in1=m2[sl][:, :],
                                                                        op=ALU.add),
                  reads=[("m1", sl), ("m2", sl)], writes=[("mg", m, t)])

    S.soft_barrier()
    A.p = markC2
    NSL = 7
    NXR = 3
    xres = [A.alloc("xres", [128, D], F32) for _ in range(NXR)]
    hres = [A.alloc("hres", [128, D], F32) for _ in range(NSL)]
    junk = [A.alloc("junk", [128, D], BF16) for _ in range(2)]
    bst = [A.alloc("bst", [128, 4], F32) for _ in range(NSL)]
    mv = [A.alloc("mv", [128, 2], F32) for _ in range(NSL)]
    nmr = [A.alloc("nmr", [128, 2], F32) for _ in range(NSL)]
    assert A.p <= endB - 2 * 4 * D * 2, (A.p, endB)

    def load_xres(n):
        xs = n % NXR
        S.add("pool", lambda e: e.dma_start(out=xres[xs][:, :], in_=x_d[n * 128:(n + 1) * 128, :]), writes=[("xres", xs)], dma=True)

    def g0(n):
        sl = n % NSL
        t = n // 4
        xs = n % NXR
        if n == 0:
            load_xres(0)
        if n + 1 < NTB:
            load_xres(n + 1)
        for hf in range(NH2):
            bank = (n * NH2 + hf) % 8

            def mo(e, hf=hf, bank=bank):
                ins = None
                for c in range(KC):
                    ins = e.matmul(ps[bank][:, :], mergedT[:, c, n * 128:(n + 1) * 128], wout[:, c, hf * 512:(hf + 1) * 512],
                                   start=(c == 0), stop=(c == KC - 1))
                return ins
            S.add("pe", mo, reads=[("mg", c, t) for c in range(KC)] + [("wout", c) for c in range(KC)], banks=[B(bank)])
            if NH2 != 2:
                S.add("dve", lambda e, hf=hf, bank=bank: e.scalar_tensor_tensor(
                    out=hres[sl][:, hf * 512:(hf + 1) * 512], in0=xres[xs][:, hf * 512:(hf + 1) * 512], scalar=ALPHA, in1=ps[bank][:, :],
                    op0=ALU.mult, op1=ALU.add, accum_out=bst[sl][:, hf:hf + 1]),
                    reads=[("xres", xs)], writes=[("hres", sl, hf), ("bst", sl, hf)], banks=[B(bank)])
        if NH2 == 2:
            b0 = (n * 2) % 8
            S.add("dve", lambda e: e.scalar_tensor_tensor(
                out=hres[sl][:, :], in0=xres[xs][:, :], scalar=ALPHA, in1=pdd[b0 // 2][:, :],
                op0=ALU.mult, op1=ALU.add, accum_out=bst[sl][:, 0:1]),
                reads=[("xres", xs)], writes=[("hres", sl, 0), ("hres", sl, 1), ("bst", sl, 0)], banks=[B(b0), B(b0 + 1)])

    def hk_(sl):
        return [("hres", sl, hf) for hf in range(NH2)]

    def g1(n):
        sl = n % NSL
        S.add("act", lambda e: e.activation(out=junk[n % 2][:, :], in_=hres[sl][:, :], func=AF.Square, accum_out=bst[sl][:, 2:3]),
              reads=hk_(sl), writes=[("junk", n % 2), ("bst", sl, 2)])

    def g2(n):
        sl = n % NSL
        if NH2 == 2:
            S.add("dve", lambda e: e.scalar_tensor_tensor(out=mv[sl][:, 1:2], in0=bst[sl][:, 0:1], scalar=1.0 / (D * D), in1=bst[sl][:, 0:1],
                                                          op0=ALU.mult, op1=ALU.mult), reads=[("bst", sl, 0)], writes=[("mv1", sl)])
            S.add("dve", lambda e: e.scalar_tensor_tensor(out=nmr[sl][:, 0:1], in0=bst[sl][:, 2:3], scalar=1.0 / D, in1=mv[sl][:, 1:2],
                                                          op0=ALU.mult, op1=ALU.subtract),
                  reads=[("bst", sl, 2), ("mv1", sl)], writes=[("nmr0", sl)])
            return
        if NH2 == 2:
            S.add("dve", lambda e: e.scalar_tensor_tensor(out=mv[sl][:, 0:1], in0=bst[sl][:, 0:1], scalar=1.0, in1=bst[sl][:, 1:2],
                                                          op0=ALU.mult, op1=ALU.add),
                  reads=[("bst", sl, 0), ("bst", sl, 1)], writes=[("mv0", sl)])
        else:
            S.add("dve", lambda e: e.tensor_copy(mv[sl][:, 0:1], bst[sl][:, 0:1]), reads=[("bst", sl, 0)], writes=[("mv0", sl)])
        S.add("dve", lambda e: e.scalar_tensor_tensor(out=mv[sl][:, 1:2], in0=mv[sl][:, 0:1], scalar=1.0 / (D * D), in1=mv[sl][:, 0:1],
                                                      op0=ALU.mult, op1=ALU.mult), reads=[("mv0", sl)], writes=[("mv1", sl)])
        S.add("dve", lambda e: e.scalar_tensor_tensor(out=nmr[sl][:, 0:1], in0=bst[sl][:, 2:3], scalar=1.0 / D, in1=mv[sl][:, 1:2],
                                                      op0=ALU.mult, op1=ALU.subtract),
              reads=[("bst", sl, 2), ("mv1", sl)], writes=[("nmr0", sl)])

    def g3(n):
        sl = n % NSL
        S.add("act", lambda e: e.activation(out=nmr[sl][:, 0:1], in_=nmr[sl][:, 0:1], func=AF.Sqrt, bias=epsb[:, 0:1]),
              reads=[("nmr0", sl), "epsb"], writes=[("nmr0", sl)])

    def g4(n):
        sl = n % NSL
        S.add("dve", lambda e: e.reciprocal(out=nmr[sl][:, 0:1], in_=nmr[sl][:, 0:1]), reads=[("nmr0", sl)], writes=[("nmr0", sl)])
        if NH2 == 2:
            return
        S.add("dve", lambda e: e.scalar_tensor_tensor(out=nmr[sl][:, 1:2], in0=mv[sl][:, 0:1], scalar=-1.0 / D, in1=nmr[sl][:, 0:1],
                                                      op0=ALU.mult, op1=ALU.mult),
              reads=[("mv0", sl), ("nmr0", sl)], writes=[("nmr1", sl)])

    def g5(n):
        sl = n % NSL
        if NH2 == 2:
            S.add("act", lambda e: e.mul(out=mv[sl][:, 0:1], in_=bst[sl][:, 0:1], mul=nmr[sl][:, 0:1]),
                  reads=[("bst", sl, 0), ("nmr0", sl)], writes=[("mv0", sl)])
            S.add("act", lambda e: e.mul(out=nmr[sl][:, 1:2], in_=mv[sl][:, 0:1], mul=-1.0 / D),
                  reads=[("mv0", sl)], writes=[("nmr1", sl)])
        S.add("act", lambda e: e.activation(out=hres[sl][:, :], in_=hres[sl][:, :], func=AF.Identity, scale=nmr[sl][:, 0:1],
                                            bias=nmr[sl][:, 1:2]),
              reads=[("nmr0", sl), ("nmr1", sl)] + hk_(sl), writes=hk_(sl))

    def g6(n):
        sl = n % NSL
        hk = hk_(sl)
        S.add("dve", lambda e: e.tensor_tensor(out=hres[sl][:, :], in0=hres[sl][:, :], in1=lng_bc[:, :], op=ALU.mult),
              reads=hk + ["lng"], writes=hk)
        S.add("dve", lambda e: e.tensor_tensor(out=hres[sl][:, :], in0=hres[sl][:, :], in1=lnb_bc[:, :], op=ALU.add),
              reads=hk + ["lnb"], writes=hk)
        S.add("sp", lambda e: e.dma_start(out=y_d[n * 128:(n + 1) * 128, :], in_=hres[sl][:, :]), reads=hk, writes=[("y", n)], dma=True)

    pipeline(NTB, [g0, g1, g2, g3, g4, g5, g6])

    S.add("sp", lambda e: e.nop(), reads=[("y", n) for n in range(NTB)])
    S.emit()
    return nc


_CACHE = {}


def kernel(**inputs):
    x = np.asarray(inputs["x"])
    Bn, T, D = x.shape
    maps = host_prep(inputs, T, D)
    key = (T, D)
    if key not in _CACHE:
        _CACHE[key] = build(T, D)
    nc = _CACHE[key]
    res = run_bass_kernel_spmd(nc, maps, core_ids=list(range(Bn)))
    out = np.stack([np.asarray(res.results[b]["y"], dtype=np.float32) for b in range(Bn)], axis=0)
    return out
```

```python
import math
from contextlib import ExitStack

import numpy as np
import concourse.bass as bass
import concourse.mybir as mybir
from concourse.ap import AP
from concourse.bass_utils import run_bass_kernel_spmd

F32 = mybir.dt.float32
BF16 = mybir.dt.bfloat16
AF = mybir.ActivationFunctionType
ALU = mybir.AluOpType

N_CORES = 8
SEQ = 2048
D_MODEL = 1024
CONV_K = 31
LN_EPS = 1e-5
LAMBDA_INIT = 0.8 - 0.6 * math.exp(-0.3 * 0)
ALPHA = (2.0 * 1) ** 0.25
NEG = -30000.0
_DBG = {}


class _Op:
    __slots__ = ("eng", "fn", "deps", "dma", "signal", "sem_key", "sig_val", "prev_ring")

    def __init__(self, eng, fn, dma):
        self.eng = eng
        self.fn = fn
        self.dma = dma
        self.deps = []
        self.signal = False
        self.sem_key = None
        self.sig_val = 0
        self.prev_ring = None


class Sched:
    ENGS = ("pe", "act", "dve", "pool", "sp")
    RING = 24

    def __init__(self, nc):
        self.nc = nc
        self.q = {e: [] for e in self.ENGS}
        self.last_write = {}
        self.readers = {}
        self.bank_last = {}
        self.phase_deps = []

    def add(self, eng, fn, reads=(), writes=(), banks=(), dma=False, after=()):
        op = _Op(eng, fn, dma)
        deps = []
        for k in after:
            lw = self.last_write.get(k)
            if lw is not None:
                deps.append(lw)
            deps.extend(self.readers.get(k, ()))
        for r in reads:
            lw = self.last_write.get(r)
            if lw is not None:
                deps.append(lw)
        for w in writes:
            lw = self.last_write.get(w)
            if lw is not None:
                deps.append(lw)
            deps.extend(self.readers.get(w, ()))
        for b in banks:
            lt = self.bank_last.get(b)
            if lt is not None and lt.eng != eng:
                deps.append(lt)
        if eng != "pe":
            deps.extend(self.phase_deps)
        seen = set()
        for p in deps:
            if id(p) in seen or p is op:
                continue
            seen.add(id(p))
            if (not p.dma) and p.eng == eng and eng == "pe":
                continue
            p.signal = True
            op.deps.append(p)
        for r in reads:
            self.readers.setdefault(r, []).append(op)
        for w in writes:
            self.last_write[w] = op
            self.readers[w] = []
        for b in banks:
            self.bank_last[b] = op
        self.q[eng].append(op)
        return op

    def soft_barrier(self):
        lasts = []
        for e in self.ENGS:
            ops = self.q[e]
            for op in reversed(ops):
                if not op.dma:
                    lasts.append(op)
                    break
            nd = 0
            for op in reversed(ops):
                if op.dma:
                    lasts.append(op)
                    nd += 1
                    if nd >= self.RING:
                        break
        for p in lasts:
            p.signal = True
        self.phase_deps = lasts

    def barrier(self):
        lasts = []
        for e in self.ENGS:
            ops = self.q[e]
            lc = None
            for op in reversed(ops):
                if not op.dma:
                    lc = op
                    break
            if lc is not None:
                lasts.append(lc)
            nd = 0
            for op in reversed(ops):
                if op.dma:
                    lasts.append(op)
                    nd += 1
                    if nd >= self.RING:
                        break
        for e in self.ENGS:
            op = _Op(e, lambda eng: eng.nop(), False)
            for p in lasts:
                if (not p.dma) and p.eng == e and e == "pe":
                    continue
                p.signal = True
                op.deps.append(p)
            self.q[e].append(op)

    def emit(self):
        nc = self.nc
        with ExitStack() as es:
            sems = {}
            for e in ("pe", "act", "dve", "pool", "sp"):
                sems[("c", e)] = es.enter_context(nc.semaphore("c_" + e))
            for e in ("sp", "pool"):
                for i in range(self.RING):
                    sems[("d", e, i)] = es.enter_context(nc.semaphore("d_%s_%d" % (e, i)))
            for e in self.ENGS:
                cnt = 0
                nd = 0
                ring_last = {}
                for op in self.q[e]:
                    if op.dma:
                        slot = nd % self.RING
                        op.sem_key = ("d", e, slot)
                        op.sig_val = 16 * (nd // self.RING + 1)
                        op.prev_ring = ring_last.get(slot)
                        ring_last[slot] = op
                        nd += 1
                    elif op.signal:
                        cnt += 1
                        op.sem_key = ("c", e)
                        op.sig_val = cnt
            block = es.enter_context(nc.Block())

            def run(ename):
                def body(eng):
                    waited = {}
                    for op in self.q[ename]:
                        need = {}
                        for p in op.deps:
                            if need.get(p.sem_key, 0) < p.sig_val:
                                need[p.sem_key] = p.sig_val
                        if op.dma and op.prev_ring is not None:
                            p = op.prev_ring
                            if need.get(p.sem_key, 0) < p.sig_val:
                                need[p.sem_key] = p.sig_val
                        for k, v in need.items():
                            if waited.get(k, 0) < v:
                                eng.wait_ge(sems[k], v)
                                waited[k] = v
                        ins = op.fn(eng)
                        if op.dma:
                            ins.then_inc(sems[op.sem_key], 16)
                        elif op.signal:
                            ins.then_inc(sems[op.sem_key], 1)

                return body

            block.tensor(run("pe"))
            block.scalar(run("act"))
            block.vector(run("dve"))
            block.gpsimd(run("pool"))
            block.sync(run("sp"))


def pipeline(n_items, stages, after_step=None):
    ns = len(stages)
    for step in range(n_items + ns - 1):
        for si in reversed(range(ns)):
            i = step - si
            if 0 <= i < n_items:
                stages[si](i)
        if after_step is not None:
            after_step(step)


class Arena:
    def __init__(self, nc, start, end):
        self.nc = nc
        self.p = start
        self.end = end
        self.n = 0

    def alloc(self, name, shape, dtype):
        esz = 2 if dtype == BF16 else 4
        size = esz
        for s in shape[1:]:
            size *= s
        size = (size + 31) // 32 * 32
        assert self.p + size <= self.end, (name, self.p, size, self.end)
        self.n += 1
        t = self.nc.alloc_sbuf_tensor_at("%s_%d" % (name, self.n), list(shape), dtype, offset=self.p)
        self.p += size
        return t

    def alloc_at(self, name, shape, dtype, offset):
        self.n += 1
        return self.nc.alloc_sbuf_tensor_at("%s_%d" % (name, self.n), list(shape), dtype, offset=offset)


def _bucket_table():
    oh = np.zeros((33, 384), np.float32)
    for m in range(383):
        n = m - 127
        if n < 0:
            oh[32, m] = NEG
            continue
        if n < 16:
            b = n
        else:
            nf = np.float32(max(n, 1))
            v = np.log(nf / np.float32(16)) / np.float32(math.log(128 / 16)) * np.float32(16)
            b = min(16 + int(np.float32(v)), 31)
        oh[b, m] = 1.0
    return oh


def host_prep(inputs, T, D):
    KC = D // 128
    f = lambda a: np.ascontiguousarray(np.asarray(a, dtype=np.float32))
    shared = {
        "w_in": f(inputs["w_in"][0]),
        "lambda_qk": f(inputs["lambda_qk"][0].reshape(1, 256)),
        "subln_w": f(inputs["subln_w"][0].reshape(1, 128)),
        "w_attn_proj": f(inputs["w_attn_proj"][0]),
        "conv_w": f(inputs["conv_w"][0].T.reshape(4, 128, CONV_K).transpose(1, 0, 2)),
        "conv_b": f(inputs["conv_b"][0].reshape(4, 128).T),
        "conv_ln_g": f(inputs["conv_ln_g"][0].reshape(4, 128).T),
        "conv_ln_b": f(inputs["conv_ln_b"][0].reshape(4, 128).T),
        "w_conv_proj": f(inputs["w_conv_proj"][0]),
        "b_conv_proj": f(inputs["b_conv_proj"][0].reshape(KC, 128).T),
        "w_out": f(inputs["w_out"][0]),
        "post_ln_g": f(inputs["post_ln_g"][0].reshape(1, D)),
        "post_ln_b": f(inputs["post_ln_b"][0].reshape(1, D)),
        "rel_bias": f(inputs["rel_bias"]),
        "onehot": _bucket_table(),
    }
    x = np.asarray(inputs["x"], dtype=np.float32)
    maps = []
    for b in range(x.shape[0]):
        m = dict(shared)
        m["x"] = np.ascontiguousarray(x[b])
        maps.append(m)
    return maps


def build(T=SEQ, D=D_MODEL):
    KC = D // 128
    NTB = T // 128
    NTT = T // 512
    NH2 = D // 512
    DIN = 3584 + 2 * D
    C_Q, C_K, C_V, C_GA, C_A, C_B, C_GC, C_G1, C_G2 = 0, 512, 1024, 1536, 2048, 2560, 3072, 3584, 3584 + D
    SCALE = 64 ** -0.5

    nc = bass.Bass("TRN2", target_bir_lowering=False)
    dt_in = lambda name, shape: nc.dram_tensor(name, list(shape), F32, kind="ExternalInput").ap()
    x_d = dt_in("x", [T, D])
    w_in_d = dt_in("w_in", [D, DIN])
    lam_d = dt_in("lambda_qk", [1, 256])
    subln_d = dt_in("subln_w", [1, 128])
    wap_d = dt_in("w_attn_proj", [512, D])
    convw_d = dt_in("conv_w", [128, 4, CONV_K])
    convb_d = dt_in("conv_b", [128, 4])
    clng_d = dt_in("conv_ln_g", [128, 4])
    clnb_d = dt_in("conv_ln_b", [128, 4])
    wcp_d = dt_in("w_conv_proj", [512, D])
    bcp_d = dt_in("b_conv_proj", [128, KC])
    wout_d = dt_in("w_out", [D, D])
    plg_d = dt_in("post_ln_g", [1, D])
    plb_d = dt_in("post_ln_b", [1, D])
    relb_d = dt_in("rel_bias", [32, 4])
    oh_d = dt_in("onehot", [33, 384])
    y_d = nc.dram_tensor("y", [T, D], F32, kind="ExternalOutput").ap()
    gscr = nc.dram_tensor("gscr", [4 * 128, 384], F32)

    S = Sched(nc)
    A = Arena(nc, 20 * 1024, 224 * 1024)

    pdd = [nc.alloc_psum_tensor("pd%d" % i, [128, 1024], F32) for i in range(4)]
    ps = [pdd[k // 2][:, (k % 2) * 512:(k % 2) * 512 + 512] for k in range(8)]
    pd = [pdd[2], pdd[3]]
    psb = [p.bitcast(BF16) for p in ps]

    def B(i):
        return ("ps", i)

    xT = A.alloc("xT", [128, KC, T], BF16)
    wbuf_off = []
    wbuf = []
    for _ in range(2):
        wbuf_off.append(A.p)
        wbuf.append(A.alloc("wbuf", [128, KC, 512], BF16))
    oaT = A.alloc("oaT", [128, 4, T], BF16)
    ucT = A.alloc("ucT", [128, 4, T], BF16)
    ident = A.alloc("ident", [128, 128], BF16)
    identf = A.alloc("identf", [128, 128], F32)
    onesdiv = A.alloc("onesdiv", [128, 128], BF16)
    lng_bc = A.alloc("lng", [128, D], F32)
    lnb_bc = A.alloc("lnb", [128, D], F32)
    subw8 = A.alloc("subw8", [128, 128], F32)
    lamt = A.alloc("lamt", [128, 256], F32)
    lamp = A.alloc("lamp", [128, 128], F32)
    lams = A.alloc("lams", [128, 8], F32)
    neglam = A.alloc("neglam", [128, 1], F32)
    mhalf = A.alloc("mhalf", [128, 8], F32)
    epsb = A.alloc("epsb", [128, 1], F32)
    convw = A.alloc("convw", [128, 4, CONV_K], F32)
    convb = A.alloc("convb", [128, 4], F32)
    clng = A.alloc("clng", [128, 4], F32)
    clnb = A.alloc("clnb", [128, 4], F32)
    bcp = A.alloc("bcp", [128, KC], F32)
    ovl = A.p

    S.add("dve", lambda e: e.memset(identf[:, :], 1.0), writes=["identf"])
    S.add("pool", lambda e: e.affine_select(out=identf[:, :], in_=identf[:, :], compare_op=ALU.is_equal, fill=0.0,
                                            base=0, pattern=[[-1, 128]], channel_multiplier=1),
          reads=["identf"], writes=["identf"])
    S.add("dve", lambda e: e.tensor_copy(ident[:, :], identf[:, :]), reads=["identf"], writes=["ident"])
    S.add("dve", lambda e: e.memset(onesdiv[:, :], 1.0 / 512.0), writes=["onesdiv"])
    S.add("dve", lambda e: e.memset(mhalf[:, :], -0.5), writes=["mhalf"])
    S.add("dve", lambda e: e.memset(epsb[:, :], LN_EPS), writes=["epsb"])
    small = [(lng_bc, plg_d.partition_broadcast(128), "lng"), (lnb_bc, plb_d.partition_broadcast(128), "lnb"),
             (subw8, subln_d.partition_broadcast(128), "subw8"), (lamt, lam_d.partition_broadcast(128), "lamt"),
             (convw, convw_d, "convw"), (convb, convb_d, "convb"), (clng, clng_d, "clng"), (clnb, clnb_d, "clnb"),
             (bcp, bcp_d, "bcp")]
    for tt, src, key in small:
        if len(tt.shape) == 3:
            S.add("sp", (lambda tt=tt, src=src: lambda e: e.dma_start(out=tt[:, :, :], in_=src[:, :, :]))(), writes=[key], dma=True)
        else:
            S.add("sp", (lambda tt=tt, src=src: lambda e: e.dma_start(out=tt[:, :], in_=src))(), writes=[key], dma=True)
    S.add("dve", lambda e: e.tensor_scalar(out=subw8[:, :], in0=subw8[:, :], scalar1=1.0 - LAMBDA_INIT, scalar2=None,
                                           op0=ALU.mult), reads=["subw8"], writes=["subw8"])
    S.add("dve", lambda e: e.tensor_tensor(out=lamp[:, 0:64], in0=lamt[:, 0:64], in1=lamt[:, 64:128], op=ALU.mult),
          reads=["lamt"], writes=["lamp0"])
    S.add("dve", lambda e: e.tensor_tensor(out=lamp[:, 64:128], in0=lamt[:, 128:192], in1=lamt[:, 192:256], op=ALU.mult),
          reads=["lamt"], writes=["lamp1"])
    S.add("dve", lambda e: e.reduce_sum(out=lams[:, 0:1], in_=lamp[:, 0:64], axis=mybir.AxisListType.X),
          reads=["lamp0"], writes=["lams0"])
    S.add("dve", lambda e: e.reduce_sum(out=lams[:, 1:2], in_=lamp[:, 64:128], axis=mybir.AxisListType.X),
          reads=["lamp1"], writes=["lams1"])
    S.add("act", lambda e: e.activation(out=lams[:, 2:4], in_=lams[:, 0:2], func=AF.Exp), reads=["lams0", "lams1"],
          writes=["lams2"])
    S.add("dve", lambda e: e.scalar_tensor_tensor(out=neglam[:, :], in0=lams[:, 3:4], scalar=-LAMBDA_INIT, in1=lams[:, 2:3],
                                                  op0=ALU.add, op1=ALU.subtract), reads=["lams2"], writes=["neglam"])

    wcount = [0]

    deferred_w = []

    def load_w(c0, ncols=512, split=1):
        i = wcount[0] % 2
        wcount[0] += 1
        step = ncols // split
        for k in range(split):
            src = w_in_d[:, c0 + k * step:c0 + (k + 1) * step].rearrange("(c p) n -> p c n", p=128)

            def issue(src=src, k=k):
                S.add("pool", lambda e: e.dma_start(out=wbuf[i][:, :, k * step:(k + 1) * step], in_=src),
                      writes=[("w", i, k)] if split > 1 else [("w", i)] + [("w", i, kk) for kk in range(4)], dma=True)
            if k == 0:
                issue()
            else:
                deferred_w.append(issue)
        return i

    evac_rr = [0]

    def evac_engine():
        evac_rr[0] += 1
        return "act" if evac_rr[0] % 2 == 0 else "dve"

    def copy_on(eng, out, in_):
        if eng == "act":
            return lambda e: e.copy(out=out, in_=in_)
        return lambda e: e.tensor_copy(out, in_)

    markA = A.p
    qT = A.alloc("qT", [128, 4, T], BF16)
    kT = A.alloc("kT", [128, 4, T], BF16)
    vaug = A.alloc("vaug", [128, NTB, 4, 130], BF16)
    sga = A.alloc("sga", [128, NTB, 512], BF16)
    NPT = 5
    pT2 = [A.alloc("pT", [128, 2, 512], BF16) for _ in range(NPT)]
    BT8 = A.alloc("BT8", [128, 4, 2, 128], BF16)
    btf = [A.alloc("btf", [128, 128], F32) for _ in range(2)]
    NXS = 4
    xbst = [A.alloc("xbst", [128, D], BF16) for _ in range(NXS)]
    relb = A.alloc("relb", [33, 4], F32)
    ohs = A.alloc("ohs", [33, 384], F32)
    lhsg = [A.alloc("lhsg", [33, 128], F32) for _ in range(4)]
    grep = [A.alloc("grep", [128, 384], F32) for _ in range(2)]
    NOB = 2
    osb = [A.alloc("osb", [128, 8, 130], F32) for _ in range(NOB)]
    rs = [A.alloc("rs", [128, 8], F32) for _ in range(NOB)]
    od = [A.alloc("od", [128, 4, 128], F32) for _ in range(NOB)]
    osq = [A.alloc("osq", [128, 128], F32) for _ in range(4)]
    osqc = [0]
    ssq = [A.alloc("ssq", [128, 4], F32) for _ in range(NOB)]
    rstd = [A.alloc("rstd", [128, 4], F32) for _ in range(NOB)]
    NOAB = 4
    oab = [A.alloc("oab", [128, 4, 128], BF16) for _ in range(NOAB)]
    tcount = [0]

    for n in range(NTB):
        S.add("dve", lambda e, n=n: e.memset(vaug[:, n, :, 128:130], 1.0), writes=[("vones", n)])

    xT_all = [("xT", n) for n in range(NTB)]

    pcount = [0]

    def proj_fm(wi, ncols, sink, tiles=None, wsplit=False):
        order = [(mi, t) for mi in range(ncols // 128) for t in range(NTT)] if tiles is None else \
                [(mi, t) for t in tiles for mi in range(ncols // 128)]
        for (mi, t) in order:
            if True:
                bank = 4 + pcount[0] % 4
                pcount[0] += 1

                def mm(e, wi=wi, mi=mi, t=t, bank=bank):
                    ins = None
                    for c in range(KC):
                        ins = e.matmul(ps[bank][:, :], wbuf[wi][:, c, mi * 128:(mi + 1) * 128], xT[:, c, t * 512:(t + 1) * 512],
                                       start=(c == 0), stop=(c == KC - 1))
                    return ins
                S.add("pe", mm, reads=[("w", wi, mi) if wsplit else ("w", wi)] + xT_all[t * 4:(t + 1) * 4], banks=[B(bank)])
                sink(mi, t, bank)

    def proj_tm(wi, sink):
        for n in range(NTB):
            bank = 4 + pcount[0] % 4
            pcount[0] += 1

            def mm(e, wi=wi, n=n, bank=bank):
                ins = None
                for c in range(KC):
                    ins = e.matmul(ps[bank][:, :], xT[:, c, n * 128:(n + 1) * 128], wbuf[wi][:, c, :],
                                   start=(c == 0), stop=(c == KC - 1))
                return ins
            S.add("pe", mm, reads=[("w", wi), ("xT", n)], banks=[B(bank)])
            sink(n, bank)

    def load_x_block(n):
        if True:
            st = xbst[n % NXS]
            S.add("pool", (lambda st=st, n=n: lambda e: e.dma_start(out=st[:, :], in_=x_d[n * 128:(n + 1) * 128, :]))(),
                  writes=[("xbst", n % NXS)], dma=True)
            bank = n % 2
            for half in range((KC + 7) // 8):
                c0 = half * 8
                cn = min(8, KC - c0)

                def tr(e, st=st, bank=bank, c0=c0, cn=cn):
                    ins = None
                    for c in range(cn):
                        ins = e.transpose(psb[bank][:, c * 128:(c + 1) * 128], st[:, (c0 + c) * 128:(c0 + c + 1) * 128], ident[:, :])
                    return ins
                S.add("pe", tr, reads=[("xbst", n % NXS), "ident"], banks=[B(bank)])
                eng = evac_engine()
                outap = xT[:, c0:c0 + cn, n * 128:(n + 1) * 128]
                inap = psb[bank][:, 0:cn * 128].rearrange("p (c t) -> p c t", c=cn)
                S.add(eng, copy_on(eng, outap, inap), writes=[("xT", n)], banks=[B(bank)])


    wi_q = load_w(C_Q, split=4)

    def sink_q(mi, t, bank):
        eng = evac_engine()
        S.add(eng, copy_on(eng, qT[:, mi, t * 512:(t + 1) * 512], ps[bank][:, :]), writes=[("qT", mi, t)], banks=[B(bank)])
    S.add("sp", lambda e: e.dma_start(out=relb[0:32, :], in_=relb_d[:, :]), writes=["relb"], dma=True)
    S.add("dve", lambda e: e.memset(relb[32:33, :], 1.0), writes=["relb1"])
    S.add("sp", lambda e: e.dma_start(out=ohs[:, :], in_=oh_d[:, :]), writes=["ohs"], dma=True)
    for h in range(4):
        S.add("dve", (lambda h=h: lambda e: e.tensor_copy(lhsg[h][:, :], relb[:, h:h + 1].to_broadcast([33, 128])))(),
              reads=["relb", "relb1"], writes=[("lhsg", h)])

    bt8_pending = []

    def flush_bt8():
        while bt8_pending:
            bt8_pending.pop(0)()

    def g_step(h):
        gb = 2 + h % 2
        flush_bt8()
        S.add("pe", lambda e: e.matmul(ps[gb][:, 0:384], lhsg[h][:, :], ohs[:, :], start=True, stop=True),
              reads=[("lhsg", h), "ohs"], banks=[B(gb)])
        S.add("dve", lambda e: e.tensor_copy(grep[h % 2][:, 383:384], ps[gb][:, 382:383]), writes=[("gfar", h % 2)], banks=[B(gb)])
        S.add("dve", lambda e: e.tensor_scalar(out=grep[h % 2][:, 0:383], in0=ps[gb][:, 0:383], scalar1=grep[h % 2][:, 383:384], scalar2=None,
                                               op0=ALU.subtract), reads=[("gfar", h % 2)], writes=[("grep", h % 2)], banks=[B(gb)])
        S.add("sp", lambda e: e.dma_start(out=gscr.ap()[h * 128:(h + 1) * 128, :], in_=grep[h % 2][:, :]),
              reads=[("grep", h % 2), ("gfar", h % 2)], writes=[("gscr", h)], dma=True)
        for dist in range(2):
            src = AP(gscr, h * 128 * 384 + dist * 128 + 127, [[383, 128], [1, 128]])
            S.add("sp", (lambda src=src, dist=dist: lambda e: e.dma_start(out=btf[dist][:, :], in_=src))(), reads=[("gscr", h)],
                  writes=[("btf", dist)], dma=True)
            bt8_pending.append((lambda dist=dist: lambda: S.add(
                "dve", lambda e: e.tensor_scalar(out=BT8[:, h, dist, :], in0=btf[dist][:, :], scalar1=8.0, scalar2=None, op0=ALU.mult),
                reads=[("btf", dist)], writes=[("BT8", h, dist)]))())

    def sink_k(mi, t, bank):
        eng = evac_engine()
        S.add(eng, copy_on(eng, kT[:, mi, t * 512:(t + 1) * 512], ps[bank][:, :]), writes=[("kT", mi, t)], banks=[B(bank)])
    wi_k = None
    for n in range(4):
        load_x_block(n)
    while deferred_w:
        deferred_w.pop(0)()
    wi_k = load_w(C_K)
    for t in range(NTT):
        proj_fm(wi_q, 512, sink_q, tiles=[t], wsplit=True)
        if t + 1 < NTT:
            for n in range(4 * (t + 1), 4 * (t + 1) + 4):
                load_x_block(n)
        proj_fm(wi_k, 512, sink_k, tiles=[t])
        if t < 4:
            g_step(t)
    for h in range(NTT, 4):
        g_step(h)
    wi_v = load_w(C_V)
    wi_ga = load_w(C_GA)

    def sink_v(n, bank):
        eng = evac_engine()
        S.add(eng, copy_on(eng, vaug[:, n, :, 0:128], ps[bank][:, :].rearrange("p (h e) -> p h e", h=4)),
              writes=[("v", n)], banks=[B(bank)])
    proj_tm(wi_v, sink_v)

    def sink_ga(n, bank):
        S.add("act", lambda e, n=n, bank=bank: e.activation(out=sga[:, n, :], in_=ps[bank][:, :], func=AF.Silu),
              writes=[("sga", n)], banks=[B(bank)])
    proj_tm(wi_ga, sink_ga)
    flush_bt8()
    wi_b = load_w(C_B)
    wi_a = load_w(C_A)

    if NTT == 4:
        torder = {0: (3, 0, 2, 1), 1: (3, 0, 2, 1), 2: (3, 0, 2, 1), 3: (3, 0, 2, 1)}
    else:
        torder = {h: tuple(reversed(range(NTT))) for h in range(4)}
    steps = [(h, t, j) for h in range(4) for t in torder[h] for j in range(4 * t + 4)]
    LOOK = 2
    ptc = [0, 0]
    info = {}
    pending = []
    deferred_tr = []
    cur_step = [0]
    ACCB = [0, 1, 2]
    TRB = 3
    NWARM = 2

    def acc_loc(c, rr):
        a = c * 4 + rr
        return ACCB[a // 3], (a % 3) * 130, (a % 3 == 0)

    def emit_scores(i):
        h, t, j = steps[i]
        r = j - 4 * t if j >= 4 * t else 0
        c0 = r * 128
        ncol = 512 - c0
        sb = [4 + (i % 2) * 2, 5 + (i % 2) * 2]

        def sc(e):
            ins = None
            for c in range(2):
                pr = slice(c * 64, (c + 1) * 64)
                ins = e.matmul(ps[sb[c]][:, 0:ncol], kT[pr, h, j * 128:(j + 1) * 128],
                               qT[pr, h, t * 512 + c0:(t + 1) * 512], start=True, stop=True)
            for c in range(2):
                for dist in range(2):
                    qb = j + dist
                    if qb < 4 * t or qb >= 4 * t + 4:
                        continue
                    off = (qb - 4 * t) * 128 - c0
                    ins = e.matmul(ps[sb[c]][:, off:off + 128], ident[:, :], BT8[:, h, dist, :], start=False, stop=True,
                                   skip_group_check=True)
            return ins
        S.add("pe", sc, reads=[("kT", h, j // 4), ("qT", h, t), ("BT8", h, 0), ("BT8", h, 1), "ident"],
              banks=[B(sb[0]), B(sb[1])])
        pi = ptc[0] % NPT
        ptc[0] += 1
        pts = [pi, pi]
        pdv = pd[i % 2][:, :].rearrange("p (c n) -> p c n", c=2)
        S.add("act", lambda e: e.activation(out=pT2[pi][:, :, 0:ncol], in_=pdv[:, :, 0:ncol], func=AF.Exp, scale=SCALE),
              writes=[("pT", 0, pi), ("pT", 1, pi)], banks=[B(sb[0]), B(sb[1])])
        info[i] = (r, c0, pts)

    def emit_pv(i):
        h, t, j = steps[i]
        r, c0, pts = info.pop(i)

        def pv(e):
            ins = None
            for c in range(2):
                for rr in range(r, 4):
                    bank, col, first = acc_loc(c, rr)
                    ins = e.matmul(ps[bank][:, col:col + 130], pT2[pts[c]][:, c, rr * 128 - c0:(rr + 1) * 128 - c0],
                                   vaug[:, j, h, :], start=(j == 0 and first), stop=(j == 4 * t + rr), skip_group_check=True)
            return ins
        if j == 0 and i > 0:
            def warm(e):
                ins = None
                for _ in range(NWARM):
                    ins = e.matmul(ps[TRB][:, :], ident[:, :], qT[:, 0, 0:512], start=True, stop=True)
                return ins
            S.add("pe", warm, reads=["ident", ("qT", 0, 0)], banks=[B(TRB)])
        S.add("pe", pv, reads=[("pT", 0, pts[0]), ("pT", 1, pts[1]), ("v", j), ("vones", j)], banks=[B(b) for b in ACCB])
        if j == 4 * t + 3:
            tile_epilogue(h, t)

    def tile_epilogue(h, t):
        ob = tcount[0] % NOB
        tb = tcount[0] % NOAB
        tcount[0] += 1
        osb_ = osb[ob]
        for bi, bank in enumerate(ACCB):
            na = 3 if bi < 2 else 2
            eng = "act"
            S.add(eng, copy_on(eng, osb_[:, 3 * bi:3 * bi + na, :], ps[bank][:, 0:na * 130].rearrange("p (a e) -> p a e", a=na)),
                  writes=[("o", ob, bi)], banks=[B(bank)])
        okeys = [("o", ob, bi) for bi in range(3)]
        chunks = []

        def c_rs():
            S.add("dve", lambda e: e.reciprocal(out=rs[ob][:, 0:8], in_=osb_[:, :, 128]), reads=okeys, writes=[("rs", ob)])
            S.add("dve", lambda e: e.tensor_scalar(out=rs[ob][:, 4:8], in0=rs[ob][:, 4:8], scalar1=neglam[:, 0:1], scalar2=None,
                                                   op0=ALU.mult), reads=[("rs", ob), "neglam"], writes=[("rs", ob)])
        chunks.append(c_rs)
        for rr in range(4):
            def c_od(rr=rr):
                S.add("dve", lambda e: e.tensor_scalar(out=od[ob][:, rr, :], in0=osb_[:, rr, 0:128], scalar1=rs[ob][:, rr:rr + 1],
                                                       scalar2=None, op0=ALU.mult),
                      reads=[("rs", ob)] + okeys, writes=[("od", ob, rr)])
                S.add("dve", lambda e: e.scalar_tensor_tensor(out=od[ob][:, rr, :], in0=osb_[:, 4 + rr, 0:128],
                                                              scalar=rs[ob][:, 4 + rr:5 + rr], in1=od[ob][:, rr, :],
                                                              op0=ALU.mult, op1=ALU.add),
                      reads=[("rs", ob), ("od", ob, rr)] + okeys, writes=[("od", ob, rr)])
                oq = osqc[0] % 4
                osqc[0] += 1
                S.add("dve", lambda e: e.scalar_tensor_tensor(out=osq[oq][:, :], in0=od[ob][:, rr, :], scalar=1.0, in1=od[ob][:, rr, :],
                                                              op0=ALU.mult, op1=ALU.mult, accum_out=ssq[ob][:, rr:rr + 1]),
                      reads=[("od", ob, rr)], writes=[("osq", oq), ("ssq", ob, rr)])
            chunks.append(c_od)

        def c_rstd():
            S.add("dve", lambda e: e.tensor_scalar(out=rstd[ob][:, :], in0=ssq[ob][:, :], scalar1=1.0 / 128.0, scalar2=LN_EPS,
                                                   op0=ALU.mult, op1=ALU.add),
                  reads=[("ssq", ob, rr) for rr in range(4)], writes=[("rstd", ob)])
            S.add("pool", lambda e: e.tensor_tensor(out=rstd[ob][:, :], in0=rstd[ob][:, :], in1=mhalf[:, 0:4], op=ALU.pow),
                  reads=[("rstd", ob), "mhalf"], writes=[("rstd", ob)])
        chunks.append(c_rstd)
        chunks.extend([None, None, None, None])
        for rr in range(4):
            def c_fin(rr=rr):
                n = 4 * t + rr
                S.add("dve", lambda e: e.scalar_tensor_tensor(out=od[ob][:, rr, :], in0=od[ob][:, rr, :], scalar=rstd[ob][:, rr:rr + 1],
                                                              in1=subw8[:, :], op0=ALU.mult, op1=ALU.mult),
                      reads=[("rstd", ob), ("od", ob, rr), "subw8"], writes=[("od", ob, rr)])
                S.add("pool", lambda e: e.tensor_tensor(out=oab[tb][:, rr, :], in0=od[ob][:, rr, :],
                                                        in1=sga[:, n, h * 128:(h + 1) * 128], op=ALU.mult),
                      reads=[("od", ob, rr), ("sga", n)], writes=[("oab", tb, rr)])
            chunks.append(c_fin)

        def c_tr():
            def tro(e):
                ins = None
                for rr in range(4):
                    ins = e.transpose(psb[TRB][:, rr * 128:(rr + 1) * 128], oab[tb][:, rr, :], ident[:, :])
                return ins
            S.add("pe", tro, reads=[("oab", tb, rr) for rr in range(4)] + ["ident"], banks=[B(TRB)])
            S.add("dve", copy_on("dve", oaT[:, h, t * 512:(t + 1) * 512], psb[TRB][:, 0:512]), writes=[("oaT", h, t)], banks=[B(TRB)])
        deferred_tr.append((cur_step[0] + 16, c_tr))
        pending.append(chunks)

    def drain(k):
        while k > 0 and pending:
            ch = pending[0]
            f = ch.pop(0)
            if f is not None:
                f()
            k -= 1
            if not ch:
                pending.pop(0)

    for i in range(len(steps) + LOOK):
        cur_step[0] = i
        if i < len(steps):
            emit_scores(i)
        if i >= LOOK:
            while deferred_tr and deferred_tr[0][0] <= i:
                deferred_tr.pop(0)[1]()
            emit_pv(i - LOOK)
            drain(6)
    drain(10 ** 6)
    upad = A.alloc_at("upad", [128, 4, T + 32], BF16, markA)
    qk_keys = [("qT", h, t) for h in range(4) for t in range(NTT)] + [("kT", h, t) for h in range(4) for t in range(NTT)]
    S.add("dve", lambda e: e.memset(upad[:, :, 0:30], 0.0), writes=["upad0"], after=qk_keys)
    diag_off = markA + ((4 * (T + 32) * 2 + 31) // 32) * 32
    diag = A.alloc_at("diag", [128, 4, CONV_K, 128], BF16, diag_off)
    diag_thunks = []
    diag_over = qk_keys + [("v", n) for n in range(NTB)] + [("vones", n) for n in range(NTB)]
    for cc in range(4):
        for j in range(CONV_K):
            if (cc * CONV_K + j) % 2 == 0:
                diag_thunks.append(lambda cc=cc, j=j: S.add(
                    "dve", lambda e: e.tensor_scalar(out=diag[:, cc, j, :], in0=identf[:, :], scalar1=convw[:, cc, j:j + 1],
                                                     scalar2=None, op0=ALU.mult),
                    reads=["identf", "convw"], writes=[("diag", cc, j)], after=diag_over))
            else:
                diag_thunks.append(lambda cc=cc, j=j: S.add(
                    "act", lambda e: e.activation(out=diag[:, cc, j, :], in_=identf[:, :], func=AF.Copy, scale=convw[:, cc, j:j + 1]),
                    reads=["identf", "convw"], writes=[("diag", cc, j)], after=diag_over))

    def sink_bb(mi, t, bank):
        S.add("act", lambda e, mi=mi, t=t, bank=bank: e.activation(out=upad[:, mi, 30 + t * 512:30 + (t + 1) * 512], in_=ps[bank][:, :],
                                                                   func=AF.Sigmoid),
              writes=[("usig", mi, t)], banks=[B(bank)], after=qk_keys)
        for _ in range(4):
            if diag_thunks:
                diag_thunks.pop(0)()
    proj_fm(wi_b, 512, sink_bb)

    def sink_a(mi, t, bank):
        S.add("dve", lambda e, mi=mi, t=t, bank=bank: e.tensor_tensor(out=upad[:, mi, 30 + t * 512:30 + (t + 1) * 512],
                                                                     in0=ps[bank][:, :], in1=upad[:, mi, 30 + t * 512:30 + (t + 1) * 512],
                                                                     op=ALU.mult),
              reads=[("usig", mi, t)], writes=[("u", mi, t)], banks=[B(bank)])
        for _ in range(4):
            if diag_thunks:
                diag_thunks.pop(0)()
    proj_fm(wi_a, 512, sink_a)

    while diag_thunks:
        diag_thunks.pop(0)()
    wi_gc = load_w(C_GC)
    while deferred_tr:
        deferred_tr.pop(0)[1]()

    S.soft_barrier()
    A.p = markA + ((4 * (T + 32) * 2 + 31) // 32) * 32

    assert A.p == diag_off
    A.p += 4 * CONV_K * 128 * 2
    ycf = [A.alloc("ycf", [128, 4, 512], F32) for _ in range(2)]
    ybf = [A.alloc("ybf", [128, 4, 512], BF16) for _ in range(2)]
    ysq = [A.alloc("ysq", [128, 4, 512], BF16) for _ in range(2)]
    mean_sb = [A.alloc("mean_sb", [128, 512], F32) for _ in range(2)]
    var_sb = [A.alloc("var_sb", [128, 512], F32) for _ in range(2)]
    rstd_c = [A.alloc("rstd_c", [128, 512], F32) for _ in range(2)]
    zt = [A.alloc("zt", [128, 512], F32) for _ in range(2)]
    wap = A.alloc("wap", [128, 4, D], BF16)
    wcp = A.alloc("wcp", [128, 4, D], BF16)
    endB = A.p
    S.add("pool", lambda e: e.dma_start(out=wap[:, :, :], in_=wap_d.rearrange("(c p) n -> p c n", p=128)), writes=["wap"], dma=True)
    S.add("pool", lambda e: e.dma_start(out=wcp[:, :, :], in_=wcp_d.rearrange("(c p) n -> p c n", p=128)), writes=["wcp"], dma=True)

    def load_wg(m):
        i = wcount[0] % 2
        wcount[0] += 1
        s1 = w_in_d[:, C_G1 + m * 128:C_G1 + (m + 1) * 128].rearrange("(c p) n -> p c n", p=128)
        s2 = w_in_d[:, C_G2 + m * 128:C_G2 + (m + 1) * 128].rearrange("(c p) n -> p c n", p=128)
        prev = [("w", i)] + [("w", i, kk) for kk in range(4)] + [("wgA", i), ("wgB", i)]
        S.add("pool", lambda e: e.dma_start(out=wbuf[i][:, :, 0:128], in_=s1), writes=[("wgA", i)], dma=True, after=prev)
        S.add("pool", lambda e: e.dma_start(out=wbuf[i][:, :, 128:256], in_=s2), writes=[("wgB", i)], dma=True, after=prev)
        return i

    wg_next = load_wg(0)

    NDT = 3

    def st0(t, inject=None):
        sl = t % 2
        for cc in range(4):
            bank = cc
            if cc == 1 and inject is not None:
                inject()

            urd = ["upad0", ("u", cc, t)] + ([("u", cc, t - 1)] if t > 0 else [])
            for j in range(NDT):
                if j == 0:
                    S.add("dve", lambda e, cc=cc: e.tensor_scalar(out=ycf[sl][:, cc, :], in0=upad[:, cc, t * 512:t * 512 + 512],
                                                                  scalar1=convw[:, cc, 0:1], scalar2=None, op0=ALU.mult),
                          reads=urd + ["convw"], writes=[("ycf", sl, cc)])
                else:
                    S.add("dve", lambda e, cc=cc, j=j: e.scalar_tensor_tensor(
                        out=ycf[sl][:, cc, :], in0=upad[:, cc, t * 512 + j:t * 512 + j + 512], scalar=convw[:, cc, j:j + 1],
                        in1=ycf[sl][:, cc, :], op0=ALU.mult, op1=ALU.add),
                        reads=urd + ["convw", ("ycf", sl, cc)], writes=[("ycf", sl, cc)])

            def cv(e, cc=cc, bank=bank):
                ins = None
                for j in range(NDT, CONV_K):
                    ins = e.matmul(ps[bank][:, :], diag[:, cc, j, :], upad[:, cc, t * 512 + j:t * 512 + j + 512],
                                   start=(j == NDT), stop=(j == CONV_K - 1))
                return ins
            rd = [("diag", cc, j) for j in range(NDT, CONV_K)] + urd
            S.add("pe", cv, reads=rd, banks=[B(bank)])
            S.add("dve", lambda e, cc=cc, bank=bank: e.scalar_tensor_tensor(out=ycf[sl][:, cc, :], in0=ps[bank][:, :],
                                                                            scalar=convb[:, cc:cc + 1], in1=ycf[sl][:, cc, :],
                                                                            op0=ALU.add, op1=ALU.add),
                  reads=["convb", ("ycf", sl, cc)], writes=[("ycf", sl, cc)], banks=[B(bank)])
            S.add("act", lambda e, cc=cc: e.copy(out=ybf[sl][:, cc, :], in_=ycf[sl][:, cc, :]), reads=[("ycf", sl, cc)],
                  writes=[("ybf", sl, cc)])
            S.add("act", lambda e, cc=cc: e.activation(out=ysq[sl][:, cc, :], in_=ycf[sl][:, cc, :], func=AF.Square),
                  reads=[("ycf", sl, cc)], writes=[("ysq", sl, cc)])

    def st1a(t):
        sl = t % 2
        b4, b5 = 4 + 2 * sl, 5 + 2 * sl

        def stats(e):
            ins = None
            for cc in range(4):
                ins = e.matmul(ps[b4][:, :], onesdiv[:, :], ybf[sl][:, cc, :], start=(cc == 0), stop=(cc == 3))
            for cc in range(4):
                ins = e.matmul(ps[b5][:, :], onesdiv[:, :], ysq[sl][:, cc, :], start=(cc == 0), stop=(cc == 3))
            return ins
        S.add("pe", stats, reads=[("ybf", sl, cc) for cc in range(4)] + [("ysq", sl, cc) for cc in range(4)] + ["onesdiv"],
              banks=[B(b4), B(b5)])

    def st1b(t):
        sl = t % 2
        b4, b5 = 4 + 2 * sl, 5 + 2 * sl
        S.add("act", lambda e: e.copy(out=mean_sb[sl][:, :], in_=ps[b4][:, :]), writes=[("mean_sb", sl)], banks=[B(b4)])
        S.add("dve", lambda e: e.tensor_tensor(out=var_sb[sl][:, :], in0=mean_sb[sl][:, :], in1=mean_sb[sl][:, :], op=ALU.mult),
              reads=[("mean_sb", sl)], writes=[("var_sb", sl)])
        S.add("dve", lambda e: e.scalar_tensor_tensor(out=var_sb[sl][:, :], in0=ps[b5][:, :], scalar=LN_EPS, in1=var_sb[sl][:, :],
                                                      op0=ALU.add, op1=ALU.subtract),
              reads=[("var_sb", sl)], writes=[("var_sb", sl)], banks=[B(b5)])
        S.add("act", lambda e: e.activation(out=var_sb[sl][:, :], in_=var_sb[sl][:, :], func=AF.Ln),
              reads=[("var_sb", sl)], writes=[("var_sb", sl)])
        S.add("act", lambda e: e.activation(out=rstd_c[sl][:, :], in_=var_sb[sl][:, :], func=AF.Exp, scale=-0.5),
              reads=[("var_sb", sl)], writes=[("rstd_c", sl)])

    def st2(t):
        sl = t % 2
        for cc in range(4):
            zi = cc % 2
            z = zt[zi]
            zk = ("zt", zi)
            S.add("dve", lambda e, cc=cc, z=z: e.tensor_tensor(out=z[:, :], in0=ycf[sl][:, cc, :], in1=mean_sb[sl][:, :], op=ALU.subtract),
                  reads=[("ycf", sl, cc), ("mean_sb", sl)], writes=[zk])
            S.add("dve", lambda e, z=z: e.tensor_tensor(out=z[:, :], in0=z[:, :], in1=rstd_c[sl][:, :], op=ALU.mult),
                  reads=[zk, ("rstd_c", sl)], writes=[zk])
            S.add("act", lambda e, cc=cc, z=z: e.activation(out=ucT[:, cc, t * 512:(t + 1) * 512], in_=z[:, :], func=AF.Silu,
                                                            scale=clng[:, cc:cc + 1], bias=clnb[:, cc:cc + 1]),
                  reads=[zk, "clng", "clnb"], writes=[("zs", cc, t)])

    gcc = [0]

    def sink_gc(mi, t, bank):
        gi = gcc[0] % 2
        gcc[0] += 1
        S.add("act", lambda e: e.activation(out=zt[gi][:, :], in_=ps[bank][:, :], func=AF.Silu), writes=[("zt", gi)], banks=[B(bank)])
        S.add("dve", lambda e: e.tensor_tensor(out=ucT[:, mi, t * 512:(t + 1) * 512], in0=ucT[:, mi, t * 512:(t + 1) * 512],
                                               in1=zt[gi][:, :], op=ALU.mult),
              reads=[("zt", gi), ("zs", mi, t)], writes=[("ucT", mi, t)])
    def gc_hook(step):
        t = step - 2
        if 0 <= t < NTT:
            proj_fm(wi_gc, 512, sink_gc, tiles=[t])
    for step in range(NTT + 2):
        if 0 <= step - 2 < NTT:
            st2(step - 2)
        if step < NTT:
            st0(step, inject=(lambda t=step - 1: st1a(t)) if step >= 1 else None)
        elif 0 <= step - 1 < NTT:
            st1a(step - 1)
        if 0 <= step - 1 < NTT:
            st1b(step - 1)
        gc_hook(step)

    S.soft_barrier()
    A.p = markA
    if _DBG.get("stop") == "B":
        S.emit()
        return nc

    mergedT = A.alloc("mergedT", [128, KC, T], BF16)
    wout = A.alloc("wout", [128, KC, D], BF16)
    markC2 = A.p
    sg1 = [A.alloc("sg1", [128, 512], F32) for _ in range(2)]
    sg2 = [A.alloc("sg2", [128, 512], F32) for _ in range(2)]
    m1 = [A.alloc("m1", [128, 512], F32) for _ in range(2)]
    m2 = [A.alloc("m2", [128, 512], F32) for _ in range(2)]

    assert A.p <= endB - 2 * 4 * D * 2, (A.p, endB)
    def load_wout():
        for c in range(KC):
            S.add("pool", lambda e, c=c: e.dma_start(out=wout[:, c, :], in_=wout_d[c * 128:(c + 1) * 128, :]), writes=[("wout", c)],
                  dma=True)

    xres_early = []
    it = 0
    for m in range(KC):
        wi = wg_next
        if m + 1 < KC:
            wg_next = load_wg(m + 1)
        if m == KC - 1 and KC >= 2 and 2 * 4 * D <= KC * 512 * 2:
            dead = 1 - wi
            prevk = [("w", dead)] + [("w", dead, kk) for kk in range(4)] + [("wgA", dead), ("wgB", dead)]
            for n_ in range(2):
                xt_ = A.alloc_at("xres", [128, D], F32, wbuf_off[dead] + n_ * 4 * D)
                xres_early.append(xt_)
                S.add("pool", (lambda xt_=xt_, n_=n_: lambda e: e.dma_start(out=xt_[:, :], in_=x_d[n_ * 128:(n_ + 1) * 128, :]))(),
                      writes=[("xres", n_)], dma=True, after=prevk)
        if m == min(2, KC - 1):
            load_wout()
        for t in range(NTT):
            bk = [(it % 2) * 4 + i for i in range(4)]
            sl = it % 2
            it += 1
            tok = slice(t * 512, (t + 1) * 512)

            def mmC(e, wi=wi, m=m, tok=tok, bk=bk):
                ins = None
                for c in range(KC):
                    ins = e.matmul(ps[bk[0]][:, :], wbuf[wi][:, c, 0:128], xT[:, c, tok], start=(c == 0), stop=(c == KC - 1))
                for c in range(KC):
                    ins = e.matmul(ps[bk[1]][:, :], wbuf[wi][:, c, 128:256], xT[:, c, tok], start=(c == 0), stop=(c == KC - 1))
                for c in range(4):
                    ins = e.matmul(ps[bk[2]][:, :], wap[:, c, m * 128:(m + 1) * 128], oaT[:, c, tok], start=(c == 0), stop=(c == 3))
                for c in range(4):
                    ins = e.matmul(ps[bk[3]][:, :], wcp[:, c, m * 128:(m + 1) * 128], ucT[:, c, tok], start=(c == 0), stop=(c == 3))
                return ins
            S.add("pe", mmC, reads=[("wgA", wi), ("wgB", wi), "wap", "wcp"] + [("oaT", c, t) for c in range(4)] + [("ucT", c, t) for c in range(4)],
                  banks=[B(b) for b in bk])
            S.add("act", lambda e, sl=sl, bk=bk: e.activation(out=sg1[sl][:, :], in_=ps[bk[0]][:, :], func=AF.Sigmoid),
                  writes=[("sg1", sl)], banks=[B(bk[0])])
            S.add("act", lambda e, sl=sl, bk=bk: e.activation(out=sg2[sl][:, :], in_=ps[bk[1]][:, :], func=AF.Sigmoid),
                  writes=[("sg2", sl)], banks=[B(bk[1])])
            S.add("dve", lambda e, sl=sl, bk=bk: e.tensor_tensor(out=m1[sl][:, :], in0=ps[bk[2]][:, :], in1=sg1[sl][:, :], op=ALU.mult),
                  reads=[("sg1", sl)], writes=[("m1", sl)], banks=[B(bk[2])])
            S.add("dve", lambda e, sl=sl, bk=bk, m=m: e.scalar_tensor_tensor(out=m2[sl][:, :], in0=ps[bk[3]][:, :], scalar=bcp[:, m:m + 1],
                                                                             in1=sg2[sl][:, :], op0=ALU.add, op1=ALU.mult),
                  reads=[("sg2", sl), "bcp"], writes=[("m2", sl)], banks=[B(bk[3])])
            S.add("pool", lambda e, sl=sl, m=m, tok=tok: e.tensor_tensor(out=mergedT[:, m, tok], in0=m1[sl][:, :], in1=m2[sl][:, :],
                                                                        op=ALU.add),
                  reads=[("m1", sl), ("m2", sl)], writes=[("mg", m, t)])

    S.soft_barrier()
    A.p = markC2
    NSL = 7
    NXR = 3
    if xres_early:
        xres = xres_early + [A.alloc("xres", [128, D], F32) for _ in range(NXR - 2)]
    else:
        xres = [A.alloc("xres", [128, D], F32) for _ in range(NXR)]
    hres = [A.alloc("hres", [128, D], F32) for _ in range(NSL)]
    junk = [A.alloc("junk", [128, D], BF16) for _ in range(2)]
    bst = [A.alloc("bst", [128, 4], F32) for _ in range(NSL)]
    mv = [A.alloc("mv", [128, 2], F32) for _ in range(NSL)]
    nmr = [A.alloc("nmr", [128, 2], F32) for _ in range(NSL)]
    assert A.p <= endB - 2 * 4 * D * 2, (A.p, endB)

    def load_xres(n):
        xs = n % NXR
        S.add("pool", lambda e: e.dma_start(out=xres[xs][:, :], in_=x_d[n * 128:(n + 1) * 128, :]), writes=[("xres", xs)], dma=True)

    def g0(n):
        sl = n % NSL
        t = n // 4
        xs = n % NXR
        if n == 0 and not xres_early:
            load_xres(0)
        if n + 1 < NTB and not (xres_early and n + 1 < 2):
            load_xres(n + 1)
        for hf in range(NH2):
            bank = (n * NH2 + hf) % 8

            def mo(e, hf=hf, bank=bank):
                ins = None
                for c in range(KC):
                    ins = e.matmul(ps[bank][:, :], mergedT[:, c, n * 128:(n + 1) * 128], wout[:, c, hf * 512:(hf + 1) * 512],
                                   start=(c == 0), stop=(c == KC - 1))
                return ins
            S.add("pe", mo, reads=[("mg", c, t) for c in range(KC)] + [("wout", c) for c in range(KC)], banks=[B(bank)])
            if NH2 != 2:
                S.add("dve", lambda e, hf=hf, bank=bank: e.scalar_tensor_tensor(
                    out=hres[sl][:, hf * 512:(hf + 1) * 512], in0=xres[xs][:, hf * 512:(hf + 1) * 512], scalar=ALPHA, in1=ps[bank][:, :],
                    op0=ALU.mult, op1=ALU.add, accum_out=bst[sl][:, hf:hf + 1]),
                    reads=[("xres", xs)], writes=[("hres", sl, hf), ("bst", sl, hf)], banks=[B(bank)])
        if NH2 == 2:
            b0 = (n * 2) % 8
            S.add("dve", lambda e: e.scalar_tensor_tensor(
                out=hres[sl][:, :], in0=xres[xs][:, :], scalar=ALPHA, in1=pdd[b0 // 2][:, :],
                op0=ALU.mult, op1=ALU.add, accum_out=bst[sl][:, 0:1]),
                reads=[("xres", xs)], writes=[("hres", sl, 0), ("hres", sl, 1), ("bst", sl, 0)], banks=[B(b0), B(b0 + 1)])

    def hk_(sl):
        return [("hres", sl, hf) for hf in range(NH2)]

    def g1(n):
        sl = n % NSL
        S.add("act", lambda e: e.activation(out=junk[n % 2][:, :], in_=hres[sl][:, :], func=AF.Square, accum_out=bst[sl][:, 2:3]),
              reads=hk_(sl), writes=[("junk", n % 2), ("bst", sl, 2)])

    def g2(n):
        sl = n % NSL
        if NH2 == 2:
            S.add("dve", lambda e: e.scalar_tensor_tensor(out=mv[sl][:, 1:2], in0=bst[sl][:, 0:1], scalar=1.0 / (D * D), in1=bst[sl][:, 0:1],
                                                          op0=ALU.mult, op1=ALU.mult), reads=[("bst", sl, 0)], writes=[("mv1", sl)])
            S.add("dve", lambda e: e.scalar_tensor_tensor(out=nmr[sl][:, 0:1], in0=bst[sl][:, 2:3], scalar=1.0 / D, in1=mv[sl][:, 1:2],
                                                          op0=ALU.mult, op1=ALU.subtract),
                  reads=[("bst", sl, 2), ("mv1", sl)], writes=[("nmr0", sl)])
            return
        if NH2 == 2:
            S.add("dve", lambda e: e.scalar_tensor_tensor(out=mv[sl][:, 0:1], in0=bst[sl][:, 0:1], scalar=1.0, in1=bst[sl][:, 1:2],
                                                          op0=ALU.mult, op1=ALU.add),
                  reads=[("bst", sl, 0), ("bst", sl, 1)], writes=[("mv0", sl)])
        else:
            S.add("dve", lambda e: e.tensor_copy(mv[sl][:, 0:1], bst[sl][:, 0:1]), reads=[("bst", sl, 0)], writes=[("mv0", sl)])
        S.add("dve", lambda e: e.scalar_tensor_tensor(out=mv[sl][:, 1:2], in0=mv[sl][:, 0:1], scalar=1.0 / (D * D), in1=mv[sl][:, 0:1],
                                                      op0=ALU.mult, op1=ALU.mult), reads=[("mv0", sl)], writes=[("mv1", sl)])
        S.add("dve", lambda e: e.scalar_tensor_tensor(out=nmr[sl][:, 0:1], in0=bst[sl][:, 2:3], scalar=1.0 / D, in1=mv[sl][:, 1:2],
                                                      op0=ALU.mult, op1=ALU.subtract),
              reads=[("bst", sl, 2), ("mv1", sl)], writes=[("nmr0", sl)])

    def g3(n):
        sl = n % NSL
        S.add("act", lambda e: e.activation(out=nmr[sl][:, 0:1], in_=nmr[sl][:, 0:1], func=AF.Sqrt, bias=epsb[:, 0:1]),
              reads=[("nmr0", sl), "epsb"], writes=[("nmr0", sl)])

    def g4(n):
        sl = n % NSL
        S.add("dve", lambda e: e.reciprocal(out=nmr[sl][:, 0:1], in_=nmr[sl][:, 0:1]), reads=[("nmr0", sl)], writes=[("nmr0", sl)])
        if NH2 == 2:
            return
        S.add("dve", lambda e: e.scalar_tensor_tensor(out=nmr[sl][:, 1:2], in0=mv[sl][:, 0:1], scalar=-1.0 / D, in1=nmr[sl][:, 0:1],
                                                      op0=ALU.mult, op1=ALU.mult),
              reads=[("mv0", sl), ("nmr0", sl)], writes=[("nmr1", sl)])

    def g5(n):
        sl = n % NSL
        if NH2 == 2:
            S.add("act", lambda e: e.mul(out=mv[sl][:, 0:1], in_=bst[sl][:, 0:1], mul=nmr[sl][:, 0:1]),
                  reads=[("bst", sl, 0), ("nmr0", sl)], writes=[("mv0", sl)])
            S.add("act", lambda e: e.mul(out=nmr[sl][:, 1:2], in_=mv[sl][:, 0:1], mul=-1.0 / D),
                  reads=[("mv0", sl)], writes=[("nmr1", sl)])
        S.add("act", lambda e: e.activation(out=hres[sl][:, :], in_=hres[sl][:, :], func=AF.Identity, scale=nmr[sl][:, 0:1],
                                            bias=nmr[sl][:, 1:2]),
              reads=[("nmr0", sl), ("nmr1", sl)] + hk_(sl), writes=hk_(sl))

    def g6(n):
        sl = n % NSL
        hk = hk_(sl)
        S.add("dve", lambda e: e.tensor_tensor(out=hres[sl][:, :], in0=hres[sl][:, :], in1=lng_bc[:, :], op=ALU.mult),
              reads=hk + ["lng"], writes=hk)
        S.add("dve", lambda e: e.tensor_tensor(out=hres[sl][:, :], in0=hres[sl][:, :], in1=lnb_bc[:, :], op=ALU.add),
              reads=hk + ["lnb"], writes=hk)
        S.add("sp", lambda e: e.dma_start(out=y_d[n * 128:(n + 1) * 128, :], in_=hres[sl][:, :]), reads=hk, writes=[("y", n)], dma=True)

    pipeline(NTB, [g0, g1, g2, g3, g4, g5, g6])

    S.add("sp", lambda e: e.nop(), reads=[("y", n) for n in range(NTB)])
    S.emit()
    return nc


_CACHE = {}


def kernel(**inputs):
    x = np.asarray(inputs["x"])
    Bn, T, D = x.shape
    maps = host_prep(inputs, T, D)
    key = (T, D)
    if key not in _CACHE:
        _CACHE[key] = build(T, D)
    nc = _CACHE[key]
    res = run_bass_kernel_spmd(nc, maps, core_ids=list(range(Bn)))
    out = np.stack([np.asarray(res.results[b]["y"], dtype=np.float32) for b in range(Bn)], axis=0)
    return out
```
